# Optimizing a Trainium2 kernel written in Bass

```python
import math
import jax, jax.numpy as jnp
from jax import lax
import numpy as np

D_MODEL = 1024
BATCH = 32
SEQ = 256
DEPTH = 2
DEC_BATCH = 8
DEC_SEQ = 2048
PAST_LEN = 256

GRID_W = 64
N_MIXERS = 4
D_GROUP = D_MODEL // N_MIXERS
N_ATT_HEADS = 4
V_DIM = D_GROUP // N_ATT_HEADS
DK = V_DIM // 2
ROPE_BASE = 10000.0
QBLOCK = 128
CONV_B_W = 3
CONV_C_W = 4
RG_BLOCKS = 4
RG_BW = D_GROUP // RG_BLOCKS
RG_C = 8.0
POOL_WINDOWS = (2, 4, 8, 16)
POOL_GW = D_GROUP // len(POOL_WINDOWS)
D_FF = 4 * D_MODEL
N_MOD = 6
N_IN_SPLITS = 9
D_IN = N_IN_SPLITS * D_GROUP
EPS = 1e-6

kernel_name = "hybrid_diffusion_parallel_heads_step"


def rmsnorm(x, w):
    xf = x.astype(jnp.float32)
    y = xf * lax.rsqrt(jnp.mean(xf * xf, axis=-1, keepdims=True) + EPS)
    return (y * w.astype(jnp.float32)).astype(x.dtype)


def rope_1d(x, pos):
    half = x.shape[-1] // 2
    freqs = ROPE_BASE ** (-jnp.arange(half, dtype=jnp.float32) / half)
    ang = pos.astype(jnp.float32)[:, None] * freqs[None, :]
    cos, sin = jnp.cos(ang), jnp.sin(ang)
    xf = x.astype(jnp.float32)
    x1, x2 = xf[..., :half], xf[..., half:]
    return jnp.concatenate([x1 * cos - x2 * sin, x2 * cos + x1 * sin], axis=-1).astype(x.dtype)


def rope_2d(x, rows, cols):
    h = x.shape[-1] // 2
    return jnp.concatenate([rope_1d(x[..., :h], rows), rope_1d(x[..., h:], cols)], axis=-1)


def rope_heads(t, rows, cols):
    return jnp.concatenate([rope_2d(t[..., :DK], rows, cols), rope_2d(t[..., DK:], rows, cols)], axis=-1)


def depthwise_conv(x, w, b, left):
    k_w = w.shape[0]
    t_len = x.shape[1]
    xp = jnp.pad(x, ((0, 0), (left, k_w - 1 - left), (0, 0)))
    out = xp[:, 0:t_len] * w[0]
    for j in range(1, k_w):
        out = out + xp[:, j:j + t_len] * w[j]
    if b is not None:
        out = out + b
    return out


def diff_attn(q, k, v, lam, lam_init, subln_w):
    bsz, n_h, t_len, _ = q.shape
    nb = t_len // QBLOCK
    qb = q.reshape(bsz, n_h, nb, QBLOCK, 2 * DK).transpose(2, 0, 1, 3, 4)
    k1, k2 = k[..., :DK], k[..., DK:]
    scale = DK ** -0.5

    def block(qi):
        s1 = jnp.einsum('bhqd,bhkd->bhqk', qi[..., :DK], k1).astype(jnp.float32) * scale
        s2 = jnp.einsum('bhqd,bhkd->bhqk', qi[..., DK:], k2).astype(jnp.float32) * scale
        p = jax.nn.softmax(s1, axis=-1) - lam * jax.nn.softmax(s2, axis=-1)
        return jnp.einsum('bhqk,bhkd->bhqd', p.astype(v.dtype), v)

    o = lax.map(block, qb)
    o = o.transpose(1, 2, 0, 3, 4).reshape(bsz, n_h, t_len, V_DIM)
    o = rmsnorm(o, subln_w) * (1.0 - lam_init)
    return o.transpose(0, 2, 1, 3).reshape(bsz, t_len, D_GROUP)


def _scan_combine(e1, e2):
    a1, b1 = e1
    a2, b2 = e2
    return (a1 * a2, a2 * b1 + b2)


def rglru(xc, w_g, b_g, lam, h0, reverse):
    bsz, t_len, ch = xc.shape
    xb = xc.reshape(bsz, t_len, RG_BLOCKS, RG_BW)
    g = jnp.einsum('btnc,kncd->kbtnd', xb, w_g).reshape(2, bsz, t_len, ch) + b_g[:, None, None, :]
    g = g.astype(jnp.float32)
    r = jax.nn.sigmoid(g[0])
    i = jax.nn.sigmoid(g[1])
    log_a = -RG_C * r * jax.nn.softplus(-lam.astype(jnp.float32))
    a = jnp.exp(log_a)
    b = jnp.sqrt(-jnp.expm1(2.0 * log_a)) * i * xc.astype(jnp.float32)
    if h0 is not None:
        idx = t_len - 1 if reverse else 0
        b = b.at[:, idx].add(a[:, idx] * h0.astype(jnp.float32))
    _, h = lax.associative_scan(_scan_combine, (a, b), reverse=reverse, axis=1)
    last = h[:, 0] if reverse else h[:, -1]
    return h.astype(xc.dtype), last.astype(xc.dtype)


def pool_mixer(x, w, scale):
    bsz, t_len, ch = x.shape
    xf = x.astype(jnp.float32)
    cs = jnp.concatenate([jnp.zeros((bsz, 1, ch), jnp.float32), jnp.cumsum(xf, axis=1)], axis=1)
    t = jnp.arange(t_len)
    outs = []
    for g, win in enumerate(POOL_WINDOWS):
        left = win // 2
        right = win - 1 - left
        lo = jnp.clip(t - left, 0, t_len - 1)
        hi = jnp.clip(t + right, 0, t_len - 1)
        csg = cs[..., g * POOL_GW:(g + 1) * POOL_GW]
        s = jnp.take(csg, hi + 1, axis=1) - jnp.take(csg, lo, axis=1)
        cnt = (hi - lo + 1).astype(jnp.float32)[None, :, None]
        outs.append(s / cnt - xf[..., g * POOL_GW:(g + 1) * POOL_GW])
    p = jnp.stack(outs, axis=2).astype(x.dtype)
    y = jnp.einsum('btgc,gcd->btgd', p, w).reshape(bsz, t_len, ch)
    return y * scale


def mixer(h, lp, lam_init, pos, kv_ctx, h0):
    bsz, t_len, _ = h.shape
    z = h @ lp['w_in']
    q, k, v, gb, gc, xb, xr, gr, xp = jnp.split(z, N_IN_SPLITS, axis=-1)

    def heads(t):
        return t.reshape(bsz, t_len, N_ATT_HEADS, V_DIM).transpose(0, 2, 1, 3)

    q, k, v = heads(q), heads(k), heads(v)
    if pos is not None:
        rows, cols = pos
        q = rope_heads(q, rows, cols)
        k = rope_heads(k, rows, cols)
    new_k, new_v = k, v
    if kv_ctx is not None:
        k = jnp.concatenate([kv_ctx[0], k], axis=2)
        v = jnp.concatenate([kv_ctx[1], v], axis=2)
    dl = lp['diff_lambda'].astype(jnp.float32)
    lam = jnp.exp(jnp.sum(dl[0] * dl[1])) - jnp.exp(jnp.sum(dl[2] * dl[3])) + lam_init
    y_a = diff_attn(q, k, v, lam, lam_init, lp['subln_w'])

    y_b = gb * depthwise_conv(gc * xb, lp['conv_b_w'], None, CONV_B_W // 2)

    xc = depthwise_conv(xr, lp['conv_c_w'], lp['conv_c_b'], CONV_C_W // 2)
    h0f = None if h0 is None else h0[:, 0]
    h0b = None if h0 is None else h0[:, 1]
    hf, hf_last = rglru(xc, lp['rg_w'][0], lp['rg_b'][0], lp['rg_lambda'][0], h0f, False)
    hb, hb_last = rglru(xc, lp['rg_w'][1], lp['rg_b'][1], lp['rg_lambda'][1], h0b, True)
    y_c = (hf + hb) * jax.nn.gelu(gr)

    y_d = pool_mixer(xp, lp['pool_w'], lp['pool_scale'])

    y = jnp.concatenate([y_a, y_b, y_c, y_d], axis=-1) @ lp['w_out']
    return y, new_k, new_v, jnp.stack([hf_last, hb_last], axis=1)


def layer(x, mod, lp, lam_init, pos, kv_ctx, h0):
    shift1, scale1, gate1, shift2, scale2, gate2 = jnp.split(mod, N_MOD, axis=-1)
    hn = rmsnorm(x, lp['norm_w'][0]) * (1.0 + scale1) + shift1
    y, k, v, hs = mixer(hn, lp, lam_init, pos, kv_ctx, h0)
    x = x + gate1 * y
    hn = rmsnorm(x, lp['norm_w'][1]) * (1.0 + scale2) + shift2
    x = x + gate2 * (jnp.square(jax.nn.relu(hn @ lp['w_mlp1'])) @ lp['w_mlp2'])
    return x, k, v, hs


def setup_inputs(seed: int = 0) -> dict:
    key = jax.random.key(seed)
    ks = jax.random.split(key, 26)
    f32 = jnp.float32
    nrm = lambda k, shape, s: (jax.random.normal(k, shape, f32) * s)
    a_init = jax.random.uniform(ks[20], (DEPTH, 2, D_GROUP), f32, 0.9, 0.999)
    return {
        'x_prompt': nrm(ks[0], (BATCH, SEQ, D_MODEL), 1.0),
        'x_sample': nrm(ks[1], (DEC_BATCH, DEC_SEQ, D_MODEL), 1.0),
        'cache_k': nrm(ks[2], (DEC_BATCH, DEPTH, N_ATT_HEADS, PAST_LEN, 2 * DK), 1.0),
        'cache_v': nrm(ks[3], (DEC_BATCH, DEPTH, N_ATT_HEADS, PAST_LEN, V_DIM), 1.0),
        'state_rglru': nrm(ks[4], (DEC_BATCH, DEPTH, 2, D_GROUP), 0.5),
        'c': nrm(ks[5], (DEC_BATCH, D_MODEL), 1.0),
        'c_ctx': nrm(ks[6], (D_MODEL,), 1.0),
        'w_ada': nrm(ks[7], (DEPTH, D_MODEL, N_MOD * D_MODEL), 0.5 * D_MODEL ** -0.5),
        'b_ada': nrm(ks[8], (DEPTH, N_MOD * D_MODEL), 0.02),
        'norm_w': 1.0 + nrm(ks[9], (DEPTH, 2, D_MODEL), 0.02),
        'w_in': nrm(ks[10], (DEPTH, D_MODEL, D_IN), D_MODEL ** -0.5),
        'diff_lambda': nrm(ks[11], (DEPTH, 4, DK), 0.1),
        'subln_w': 1.0 + nrm(ks[12], (DEPTH, V_DIM), 0.02),
        'conv_b_w': nrm(ks[13], (DEPTH, CONV_B_W, D_GROUP), CONV_B_W ** -0.5),
        'conv_c_w': nrm(ks[14], (DEPTH, CONV_C_W, D_GROUP), CONV_C_W ** -0.5),
        'conv_c_b': nrm(ks[15], (DEPTH, D_GROUP), 0.02),
        'rg_w': nrm(ks[16], (DEPTH, 2, 2, RG_BLOCKS, RG_BW, RG_BW), RG_BW ** -0.5),
        'rg_b': nrm(ks[17], (DEPTH, 2, 2, D_GROUP), 0.02),
        'rg_lambda': jnp.log(a_init) - jnp.log1p(-a_init),
        'pool_w': nrm(ks[18], (DEPTH, len(POOL_WINDOWS), POOL_GW, POOL_GW), POOL_GW ** -0.5),
        'pool_scale': 1.0 + nrm(ks[19], (DEPTH, D_GROUP), 0.1),
        'w_out': nrm(ks[21], (DEPTH, D_MODEL, D_MODEL), D_MODEL ** -0.5),
        'w_mlp1': nrm(ks[22], (DEPTH, D_MODEL, D_FF), D_MODEL ** -0.5),
        'w_mlp2': nrm(ks[23], (DEPTH, D_FF, D_MODEL), D_FF ** -0.5),
        'final_norm_w': 1.0 + nrm(ks[24], (D_MODEL,), 0.02),
    }


def reference(x_prompt, x_sample, cache_k, cache_v, state_rglru, c, c_ctx, w_ada, b_ada, norm_w,
              w_in, diff_lambda, subln_w, conv_b_w, conv_c_w, conv_c_b, rg_w, rg_b, rg_lambda,
              pool_w, pool_scale, w_out, w_mlp1, w_mlp2, final_norm_w):
    t_lat = x_sample.shape[1]
    n_rows = t_lat // GRID_W
    rows = jnp.repeat(jnp.arange(n_rows), GRID_W)
    cols = jnp.tile(jnp.arange(GRID_W), n_rows)
    pos = (rows, cols)

    xp = x_prompt
    xs = x_sample
    ks_out, vs_out, hs_out = [], [], []
    for l in range(DEPTH):
        lp = dict(w_in=w_in[l], diff_lambda=diff_lambda[l], subln_w=subln_w[l],
                  conv_b_w=conv_b_w[l], conv_c_w=conv_c_w[l], conv_c_b=conv_c_b[l],
                  rg_w=rg_w[l], rg_b=rg_b[l], rg_lambda=rg_lambda[l], pool_w=pool_w[l],
                  pool_scale=pool_scale[l], w_out=w_out[l], norm_w=norm_w[l],
                  w_mlp1=w_mlp1[l], w_mlp2=w_mlp2[l])
        lam_init = 0.8 - 0.6 * math.exp(-0.3 * l)
        mod_ctx = (jax.nn.silu(c_ctx) @ w_ada[l] + b_ada[l])[None, None, :]
        mod_lat = (jax.nn.silu(c) @ w_ada[l] + b_ada[l])[:, None, :]
        xp, k_new, v_new, h_new = layer(xp, mod_ctx, lp, lam_init, None, None, None)
        ks_out.append(k_new)
        vs_out.append(v_new)
        hs_out.append(h_new)
        xs, _, _, _ = layer(xs, mod_lat, lp, lam_init, pos,
                            (cache_k[:, l], cache_v[:, l]), state_rglru[:, l])
    y_prompt = rmsnorm(xp, final_norm_w)
    y_sample = rmsnorm(xs, final_norm_w)
    new_cache_k = jnp.stack(ks_out, axis=1)
    new_cache_v = jnp.stack(vs_out, axis=1)
    new_state_rglru = jnp.stack(hs_out, axis=1)
    return (y_prompt, y_sample, new_cache_k, new_cache_v, new_state_rglru)
```

```python
import math
from contextlib import ExitStack

import numpy as np
import concourse.bass as bass
import concourse.mybir as mybir
from concourse.bass_utils import run_bass_kernel_spmd

F32 = mybir.dt.float32
BF16 = mybir.dt.bfloat16
AF = mybir.ActivationFunctionType
ALU = mybir.AluOpType

D = 1024
KC = 8
DEPTH = 2
EPS = 1e-6
NB_IN = 22
GELU_K = math.sqrt(2.0 / math.pi)

OX = 0
OHN = 65536
OY6 = 98304
OKT = 122880
OVA = 132096
OQT = 141568
OSCR = 149760
SLOT = 8448
OWS = 192000
OWV = 198144
OCB = 202240
ARENA_BYTES = 212736
C_IDF = 0
C_ONES = 512
C_SP = 768
C_MOD = 3584
C_G = 4352
C_SC = 4608
C_MISC = 4864
C_WSM = 5376
C_HST = 7936
C_ATT = 8192
NSP_MAX = 704

SP = {}
_o = 0
for _n, _w in [("c", 8), ("cctx", 8), ("bada", 96), ("normw", 32), ("fnw", 8), ("convb", 12), ("convc", 16),
               ("convcb", 4), ("rgb", 16), ("rglam", 8), ("pscale", 4), ("h0", 8), ("pinv", 2), ("pedge", 64),
               ("dl", 256), ("wsub", 128)]:
    SP[_n] = _o
    _o += _w
NSP = _o
assert NSP <= NSP_MAX


class _Op:
    __slots__ = ("eng", "fn", "deps", "is_dma", "dsem", "ticket", "signal", "dmawaits")

    def __init__(self, eng, fn, is_dma, dsem):
        self.eng = eng
        self.fn = fn
        self.deps = []
        self.is_dma = is_dma
        self.dsem = dsem
        self.ticket = 0
        self.signal = False
        self.dmawaits = {}


class Sched:
    def __init__(self):
        self.ops = []
        self.lastw = {}
        self.readers = {}
        self.dcount = {}
        self.dbg = []

    def add(self, eng, fn, reads=(), writes=(), dsem=None):
        if _MUTE["on"]:
            return -1
        idx = len(self.ops)
        is_dma = dsem is not None
        pr = [k for k in reads if k[0] == "ps"]
        if pr:
            writes = list(writes) + pr
            reads = [k for k in reads if k[0] != "ps"]
        op = _Op(eng, fn, is_dma, dsem)
        if _DEBUG:
            import sys as _sys
            fr = _sys._getframe(1)
            while fr.f_code.co_name in ("<lambda>", "add", "PE", "ACT", "DVE", "POOL", "DMA"):
                fr = fr.f_back
            self.dbg.append((fr.f_lineno, list(reads), list(writes)))
        need = {}
        for k in reads:
            w = self.lastw.get(k)
            if w is not None:
                need[w] = need.get(w, False) or True
        for k in writes:
            w = self.lastw.get(k)
            if w is not None:
                need.setdefault(w, False)
            rd = self.readers.get(k)
            if rd:
                for r in rd.values():
                    need.setdefault(r, False)
        for d, raw in need.items():
            od = self.ops[d]
            if od.is_dma:
                op.dmawaits[od.dsem] = 16 * self.dcount[od.dsem]
            elif od.eng != eng or is_dma:
                od.signal = True
                op.deps.append(d)
            elif raw and eng in ("act", "dve", "pool"):
                od.signal = True
                op.deps.append(d)
        if is_dma:
            self.dcount[dsem] = self.dcount.get(dsem, 0) + 1
            op.ticket = 16 * self.dcount[dsem]
        for k in writes:
            self.lastw[k] = idx
            self.readers[k] = {}
        rkey = ("dma", dsem) if is_dma else eng
        for k in reads:
            self.readers.setdefault(k, {})[rkey] = idx
        self.ops.append(op)
        return idx

    def emit(self, nc, es):
        engs = ["pe", "act", "dve", "pool"]
        cnt = {e: 0 for e in engs}
        for op in self.ops:
            if not op.is_dma and op.signal:
                cnt[op.eng] += 1
                op.ticket = cnt[op.eng]
        for e in engs:
            assert cnt[e] < 60000, (e, cnt[e])
        for s, c in self.dcount.items():
            assert 16 * c < 60000, (s, c)
        sem = {e: es.enter_context(nc.semaphore("s_" + e)) for e in engs}
        dsem = {s: es.enter_context(nc.semaphore("d_" + s)) for s in self.dcount}
        block = es.enter_context(nc.Block())
        ops = self.ops
        final = dict((s, 16 * c) for s, c in self.dcount.items())

        def stream(ename):
            def run(e):
                waited = {}
                for op in ops:
                    if op.eng != ename:
                        continue
                    for d in op.deps:
                        od = ops[d]
                        key = ("e", od.eng)
                        if waited.get(key, 0) < od.ticket:
                            e.wait_ge(sem[od.eng], od.ticket)
                            waited[key] = od.ticket
                    for s, v in op.dmawaits.items():
                        key = ("d", s)
                        if waited.get(key, 0) < v:
                            e.wait_ge(dsem[s], v)
                            waited[key] = v
                    ins = op.fn(e)
                    if op.is_dma:
                        ins.then_inc(dsem[op.dsem], 16)
                    elif op.signal:
                        ins.then_inc(sem[op.eng], 1)
                if ename == "sp":
                    for s, v in final.items():
                        e.wait_ge(dsem[s], v)
            return run

        block.tensor(stream("pe"))
        block.scalar(stream("act"))
        block.vector(stream("dve"))
        block.gpsimd(stream("pool"))
        block.sync(stream("sp"))


def _f_mm(out, lhsT, rhs, start, stop, **kw):
    return lambda e: e.matmul(out=out, lhsT=lhsT, rhs=rhs, start=start, stop=stop, **kw)


def _f_tr(out, in_, ident):
    return lambda e: e.transpose(out=out, in_=in_, identity=ident)


def _f_act(out, in_, func, bias=None, scale=None, accum_out=None):
    kw = {}
    if bias is not None:
        kw["bias"] = bias
    if scale is not None:
        kw["scale"] = scale
    if accum_out is not None:
        kw["accum_out"] = accum_out
    return lambda e: e.activation(out=out, in_=in_, func=func, **kw)


def _f_copy(out, in_):
    return lambda e: e.tensor_copy(out=out, in_=in_)


def _f_acopy(out, in_):
    return lambda e: e.copy(out=out, in_=in_)


def _f_tt(out, in0, in1, op):
    return lambda e: e.tensor_tensor(out=out, in0=in0, in1=in1, op=op)


def _f_ts(out, in0, s1, op0, s2=None, op1=None):
    if op1 is None:
        return lambda e: e.tensor_scalar(out=out, in0=in0, scalar1=s1, scalar2=None, op0=op0)
    return lambda e: e.tensor_scalar(out=out, in0=in0, scalar1=s1, scalar2=s2, op0=op0, op1=op1)


def _f_stt(out, in0, scalar, in1, op0, op1):
    return lambda e: e.scalar_tensor_tensor(out=out, in0=in0, scalar=scalar, in1=in1, op0=op0, op1=op1)


def _f_scan(out, d0, d1, init):
    return lambda e: e.tensor_tensor_scan(out=out, data0=d0, data1=d1, initial=init, op0=ALU.mult, op1=ALU.add)


def _f_dma(out, in_):
    return lambda e: e.dma_start(out=out, in_=in_)


def _f_memset(ap, val):
    return lambda e: e.memset(ap, val)


def _f_recip(out, in_):
    return lambda e: e.reciprocal(out=out, in_=in_)


def build_program():
    nc = bass.Bass("TRN2", target_bir_lowering=False)
    S = Sched()

    def din(name, shape, dt=F32):
        return nc.dram_tensor(name, list(shape), dt, kind="ExternalInput").ap()

    def dout(name, shape):
        return nc.dram_tensor(name, list(shape), F32, kind="ExternalOutput").ap()

    def dint(name, shape):
        return nc.dram_tensor(name, list(shape), BF16, kind="Internal").ap()

    xs_d = din("xs", [2048, D])
    xp_d = din("xp", [1024, D])
    ck_d = din("ck", [DEPTH, 4, 256, 64])
    cv_d = din("cv", [DEPTH, 4, 256, 64])
    sp_d = din("sp", [128, NSP])
    ropec_d = din("ropec", [128, 2048])
    ropes_d = din("ropes", [128, 2048])
    ident_d = din("ident", [128, 128])
    wada_d = din("wada", [DEPTH, 12, 128, KC, 512])
    win_d = din("win", [DEPTH, NB_IN * 128, KC * 128])
    wkv_d = din("wkv", [DEPTH, 128, KC * 512])
    wsm_d = din("wsm", [DEPTH, 128, 10 * 128])
    wout_d = din("wout", [DEPTH, 128, KC * D])
    w1_d = din("w1", [DEPTH, 32 * 128, KC * 128])
    w2_d = din("w2", [DEPTH, 32 * 128, D])
    winb = dint("winb", [DEPTH, NB_IN * 128, KC * 128])
    wkvb = dint("wkvb", [DEPTH, 128, KC * 512])
    wsmb = dint("wsmb", [DEPTH, 128, 10 * 128])
    woutb = dint("woutb", [DEPTH, 128, KC * D])
    w1b = dint("w1b", [DEPTH, 32 * 128, KC * 128])
    w2b = dint("w2b", [DEPTH, 32 * 128, D])
    yp_d = dout("yp", [1024, D])
    ys_d = dout("ys", [2048, D])
    nck_d = dout("nck", [4, DEPTH, 4, 256, 64])
    ncv_d = dout("ncv", [4, DEPTH, 4, 256, 64])
    nst_d = dout("nst", [32, 128])

    es = ExitStack()
    with es:
        A = es.enter_context(nc.sbuf_tensor("arena", [128, ARENA_BYTES // 4], F32))
        PP = [es.enter_context(nc.psum_tensor("pp%d" % i, [128, 1024], F32)) for i in range(4)]

        def fv(off, n):
            assert off % 4 == 0
            return A[:, off // 4: off // 4 + n]

        def bv(off, n):
            assert off % 4 == 0 and n % 2 == 0
            return A[:, off // 4: off // 4 + n // 2].bitcast(BF16)

        def K(off, nbytes):
            return [("A", g) for g in range(off // 256, (off + nbytes - 1) // 256 + 1)]

        def bank(b):
            return PP[b // 2][:, (b % 2) * 512:(b % 2) * 512 + 512]

        def BK(b):
            return [("ps", b)]

        rr = {"g": 0, "set": [0, 1, 2, 3, 4, 5]}

        def nb():
            st_ = rr["set"]
            rr["g"] = (rr["g"] + 1) % len(st_)
            return st_[rr["g"]]

        PE = lambda fn, r, w: S.add("pe", fn, r, w)
        ACT = lambda fn, r, w: S.add("act", fn, r, w)
        DVE = lambda fn, r, w: S.add("dve", fn, r, w)
        POOL = lambda fn, r, w: S.add("dve", fn, r, w)
        DMA = lambda fn, r, w, ds, q="sp": S.add(q, fn, r, w, dsem=ds)

        IDF = fv(OCB + C_IDF, 128)
        K_IDF = K(OCB + C_IDF, 512)
        ONESB = bv(OCB + C_ONES, 128)
        K_ONES = K(OCB + C_ONES, 256)
        SPV = fv(OCB + C_SP, NSP)
        K_SP = K(OCB + C_SP, NSP * 4)
        MODV = fv(OCB + C_MOD, 192)
        K_MOD = K(OCB + C_MOD, 768)
        GV = fv(OCB + C_G, 64)
        K_G = K(OCB + C_G, 256)
        SCV = fv(OCB + C_SC, 16)
        K_SC = K(OCB + C_SC, 64)
        MISC = fv(OCB + C_MISC, 128)
        K_MISC = K(OCB + C_MISC, 512)
        WSMV = bv(OCB + C_WSM, 1280)
        K_WSM = K(OCB + C_WSM, 2560)
        HSTV = fv(OCB + C_HST, 32)
        K_HST = K(OCB + C_HST, 128)
        ATT = fv(OCB + C_ATT, 128)
        K_ATT = K(OCB + C_ATT, 512)

        def spc(name, i):
            o = SP[name] + i
            return SPV[:, o:o + 1]

        def modc(l, m, c, j):
            o = (l * 48 + m * 8 + c) * 2 + j
            return MODV[:, o:o + 1]

        def gc_(l, n, c, j):
            o = ((l * 2 + n) * 8 + c) * 2 + j
            return GV[:, o:o + 1]

        def misc(i):
            return MISC[:, i:i + 1]

        cch = {"i": 0}

        def cast(src, dst, rows, cols, name):
            per = max(1, (4 << 20) // (cols * 4))
            r0 = 0
            while r0 < rows:
                r1 = min(rows, r0 + per)
                cch["i"] += 1
                DMA(_f_dma(dst[r0:r1, :], src[r0:r1, :]), [], [("dram", name), ("castchain", cch["i"] % 2)],
                    "c_" + name, q="pool")
                r0 = r1

        def cast_layer(l):
            cast(win_d[l], winb[l], NB_IN * 128, KC * 128, "win%d" % l)
            cast(wkv_d[l].rearrange("p (a b) -> (p a) b", a=4), wkvb[l].rearrange("p (a b) -> (p a) b", a=4),
                 512, 1024, "wkv%d" % l)
            cast(wsm_d[l], wsmb[l], 128, 1280, "wsm%d" % l)
            cast(wout_d[l].rearrange("p (a b) -> (p a) b", a=8), woutb[l].rearrange("p (a b) -> (p a) b", a=8),
                 1024, 1024, "wout%d" % l)
            cast(w1_d[l], w1b[l], 32 * 128, KC * 128, "w1%d" % l)
            cast(w2_d[l], w2b[l], 32 * 128, D, "w2%d" % l)

        _stage(-5)
        DMA(_f_dma(IDF, ident_d[:, :]), [], K_IDF, "cst")
        DMA(_f_dma(SPV, sp_d[:, :]), [], K_SP, "cst")
        DVE(_f_memset(ONESB, 1.0), [], K_ONES)

        _stage(-4)
        SCB = bv(OCB + C_SC + 64, 16)
        K_SCB = K(OCB + C_SC + 64, 32)
        scb3 = SCB.rearrange("p (k j) -> p k j", j=2)
        ACT(_f_act(scb3[:, :, 0], SPV[:, SP["cctx"]:SP["cctx"] + 8], AF.Silu), K_SP, K_SCB)
        ACT(_f_act(scb3[:, :, 1], SPV[:, SP["c"]:SP["c"] + 8], AF.Silu), K_SP, K_SCB)
        modv3 = MODV.rearrange("p (a j) -> p a j", j=2)
        gv4 = GV.rearrange("p (a c j) -> p a c j", c=8, j=2)

        def setup_layer(l, st_offs, mb):
            for blk in range(12):
                off = st_offs[blk % len(st_offs)]
                wst3 = bv(off, KC * 512).rearrange("p (k n) -> p k n", k=KC)
                DMA(_f_dma(wst3, wada_d[l, blk]), [], K(off, 8192), "wada%d" % (blk % len(st_offs)), q="pool")
                for cb in range(4):
                    col = (blk * 4 + cb) * 2
                    for kc in range(KC):
                        PE(_f_mm(bank(mb)[:, col:col + 2], wst3[:, kc, cb * 128:(cb + 1) * 128],
                                 scb3[:, kc, :], kc == 0, kc == KC - 1),
                           K(off, 8192) + K_SCB, BK(mb))
            modp3 = bank(mb)[:, 0:96].rearrange("p (a j) -> p a j", j=2)
            for j in range(2):
                DVE(_f_tt(modv3[:, l * 48:(l + 1) * 48, j], modp3[:, :, j],
                          SPV[:, SP["bada"] + l * 48:SP["bada"] + l * 48 + 48], ALU.add),
                    BK(mb) + K_SP, K_MOD)
            for n in range(2):
                m = 1 if n == 0 else 4
                for j in range(2):
                    a = l * 2 + n
                    DVE(_f_stt(gv4[:, a, :, j], modv3[:, l * 48 + m * 8:l * 48 + m * 8 + 8, j], 1.0,
                               SPV[:, SP["normw"] + a * 8:SP["normw"] + a * 8 + 8], ALU.add, ALU.mult),
                        K_MOD + K_SP, K_G)
            lam_init = 0.8 - 0.6 * math.exp(-0.3 * l)
            base = l * 16
            dl = SPV[:, SP["dl"] + l * 128:SP["dl"] + l * 128 + 128]
            tmp = ATT[:, 64:128]
            k_t = K(OCB + C_ATT + 256, 256)
            for t_ in range(2):
                DVE(_f_tt(tmp[:, t_ * 32:t_ * 32 + 32], dl[:, t_ * 64:t_ * 64 + 32], dl[:, t_ * 64 + 32:t_ * 64 + 64],
                          ALU.mult), K_SP, k_t)
                DVE(lambda e, o=misc(base + t_), i_=tmp[:, t_ * 32:t_ * 32 + 32]:
                    e.tensor_reduce(out=o, in_=i_, axis=mybir.AxisListType.X, op=ALU.add), k_t, K_MISC)
                ACT(_f_act(misc(base + 2 + t_), misc(base + t_), AF.Exp), K_MISC, K_MISC)
            DVE(_f_stt(misc(base + 4), misc(base + 2), lam_init, misc(base + 3), ALU.add, ALU.subtract),
                K_MISC, K_MISC)
            DVE(_f_ts(misc(base + 5), misc(base + 4), -1.0, ALU.mult), K_MISC, K_MISC)
            ws_ = SPV[:, SP["wsub"] + l * 64:SP["wsub"] + l * 64 + 64]
            DVE(_f_ts(ws_, ws_, 1.0 - lam_init, ALU.mult), K_SP, K_SP)
            lamv = SPV[:, SP["rglam"] + l * 4:SP["rglam"] + l * 4 + 4]
            clv = MISC[:, 32 + l * 4:32 + l * 4 + 4]
            cl2v = MISC[:, 48 + l * 4:48 + l * 4 + 4]
            ACT(_f_act(clv, lamv, AF.Exp, scale=-1.0), K_SP, K_MISC)
            ACT(_f_act(clv, clv, AF.Ln, bias=1.0), K_MISC, K_MISC)
            DVE(_f_ts(cl2v, clv, -16.0, ALU.mult), K_MISC, K_MISC)
            DVE(_f_ts(clv, clv, -8.0, ALU.mult), K_MISC, K_MISC)

        _stage(-3)
        setup_layer(0, [OSCR + i * SLOT for i in range(4)], 7)
        cast_layer(0)
        cast_layer(1)
        _stage(-1)

        def run_batch(is_sample):
            nseq = 1 if is_sample else 4
            T = 2048 if is_sample else 256
            NT = nseq * T
            NTL = NT // 512
            past = 256 if is_sample else 0
            jm = 1 if is_sample else 0
            x_d = xs_d if is_sample else xp_d
            y_d = ys_d if is_sample else yp_d
            TK = past + T
            NK = TK // 128
            KTW = nseq * TK
            tag = "s" if is_sample else "p"

            XV = fv(OX, 8 * NT).rearrange("p (c t) -> p c t", c=8)
            HNV = bv(OHN, 8 * NT).rearrange("p (c t) -> p c t", c=8)
            Y6V = bv(OY6, 6 * NT).rearrange("p (c t) -> p c t", c=6)
            KTV = bv(OKT, 2 * KTW).rearrange("p (c t) -> p c t", c=2)
            NKT = nseq * NK
            VAF = bv(OVA, NKT * 260)
            VAV = VAF.rearrange("p (k h d) -> p k h d", h=4, d=65)
            QTV = bv(OQT, 2 * NT).rearrange("p (c t) -> p c t", c=2)
            YAV = bv(OSCR, 2 * NT).rearrange("p (c t) -> p c t", c=2)

            def XK(c, t0, t1):
                return K(OX + (c * NT + t0) * 4, (t1 - t0) * 4)

            def HNK(c, t0, t1):
                return K(OHN + (c * NT + t0) * 2, (t1 - t0) * 2)

            def Y6K(c, t0, t1):
                return K(OY6 + (c * NT + t0) * 2, (t1 - t0) * 2)

            def KTK(c, t0, t1):
                return K(OKT + (c * KTW + t0) * 2, (t1 - t0) * 2)

            def VAK(k0, k1):
                return K(OVA + k0 * 520, (k1 - k0) * 520)

            def QTK(c, t0, t1):
                return K(OQT + (c * NT + t0) * 2, (t1 - t0) * 2)

            def YAK(c, t0, t1):
                return K(OSCR + (c * NT + t0) * 2, (t1 - t0) * 2)

            def slot_f(i, n=None):
                return fv(OSCR + i * SLOT, SLOT // 4 if n is None else n)

            def SK(i, c0=0, c1=None, es_=4):
                if c1 is None:
                    return K(OSCR + i * SLOT, SLOT)
                return K(OSCR + i * SLOT + c0 * es_, (c1 - c0) * es_)

            _stage(1 + (50 if is_sample else 0))
            for i in range(NT // 128):
                soff = OSCR + 4 * SLOT + (i % 2) * 4096
                st = fv(soff, 1024)
                DMA(_f_dma(st, x_d[i * 128:(i + 1) * 128, :]), [], K(soff, 4096), "xs%d" % (i % 2))
                for half in range(2):
                    b = nb()
                    for q in range(4):
                        c = half * 4 + q
                        PE(_f_tr(bank(b)[:, q * 128:(q + 1) * 128], st[:, c * 128:(c + 1) * 128], IDF),
                           K(soff, 4096) + K_IDF, BK(b))
                    outv = XV[:, half * 4:half * 4 + 4, i * 128:(i + 1) * 128]
                    inv = bank(b).rearrange("p (a t) -> p a t", a=4)
                    wk = []
                    for c in range(half * 4, half * 4 + 4):
                        wk += XK(c, i * 128, (i + 1) * 128)
                    if (2 * i + half) % 2 == 0:
                        ACT(_f_acopy(outv, inv), BK(b), wk)
                    else:
                        DVE(_f_copy(outv, inv), BK(b), wk)

            def norm_tile(j, gfn, shfn, out_fn, tmp_off, pe_pre=None, pe_post=None):
                t0, t1 = j * 512, (j + 1) * 512
                sq = bv(tmp_off, 8 * 512).rearrange("p (c t) -> p c t", c=8)
                k_sq = K(tmp_off, 8192)
                xk = []
                for c in range(8):
                    xk += XK(c, t0, t1)
                ACT(_f_act(sq, XV[:, :, t0:t1], AF.Square), xk, k_sq)
                if pe_pre is not None:
                    pe_pre()
                b = nb()
                for c in range(8):
                    PE(_f_mm(bank(b), ONESB, sq[:, c, :], c == 0, c == 7), K_ONES + k_sq, BK(b))
                if pe_post is not None:
                    pe_post()
                ln = fv(tmp_off + 8192, 512)
                k_ln = K(tmp_off + 8192, 2048)
                rs = fv(tmp_off + 10240, 512)
                k_rs = K(tmp_off + 10240, 2048)
                ACT(_f_act(ln, bank(b), AF.Ln, bias=EPS, scale=1.0 / D), BK(b), k_ln)
                ACT(_f_act(rs, ln, AF.Exp, scale=-0.5), k_ln, k_rs)
                for c in range(8):
                    to = tmp_off + 12288 + (c % 2) * 2048
                    tt = fv(to, 512)
                    DVE(_f_stt(tt, XV[:, c, t0:t1], gfn(c), rs, ALU.mult, ALU.mult),
                        XK(c, t0, t1) + k_rs + K_G + K_SP, K(to, 2048))
                    out_fn(c, tt, K(to, 2048), shfn(c) if shfn is not None else None)

            wsr = {"i": 0}

            def load_wblock(l, blk):
                i = wsr["i"]
                wsr["i"] = (i + 1) % 3
                off = OWS + i * 2048
                v = bv(off, KC * 128).rearrange("p (k n) -> p k n", k=KC)
                DMA(_f_dma(v, winb[l, blk * 128:(blk + 1) * 128, :].rearrange("p (k n) -> p k n", k=KC)),
                    [("dram", "win%d" % l)], K(off, 2048), "ws%d" % i)
                return v, K(off, 2048)

            def proj_tile(wv, wk, j):
                b = nb()
                for kc in range(KC):
                    PE(_f_mm(bank(b), wv[:, kc, :], HNV[:, kc, j * 512:(j + 1) * 512], kc == 0, kc == KC - 1),
                       wk + HNK(kc, j * 512, (j + 1) * 512), BK(b))
                return b

            def seq_pieces(j):
                out = []
                g0 = j * 512
                while g0 < (j + 1) * 512:
                    s = g0 // T
                    g1 = min((s + 1) * T, (j + 1) * 512)
                    out.append((s, g0 - s * T, g1 - s * T, g0 - j * 512))
                    g0 = g1
                return out

            for l in range(min(DEPTH, _KLAYERS)):
                rr["set"] = [0, 1, 2, 3, 4, 5]
                lam_init = 0.8 - 0.6 * math.exp(-0.3 * l)
                DMA(_f_dma(WSMV, wsmb[l]), [("dram", "wsm%d" % l)], K_WSM, "wsm")

                def wsm(i):
                    return WSMV[:, i * 128:(i + 1) * 128]

                _stage(2 + 10 * l + (50 if is_sample else 0))
                def hn_out(j):
                    def f(c, tt, tk, sh):
                        DVE(_f_ts(HNV[:, c, j * 512:(j + 1) * 512], tt, sh, ALU.add), tk + K_MOD,
                            HNK(c, j * 512, (j + 1) * 512))
                    return f

                for j in range(NTL):
                    norm_tile(j, lambda c: gc_(l, 0, c, jm), lambda c: modc(l, 0, c, jm), hn_out(j), OSCR)

                _stage(3 + 10 * l + (50 if is_sample else 0))
                if is_sample:
                    rc_off, rs_off = OSCR + 3 * SLOT, OSCR + 4 * SLOT
                    RC = fv(rc_off, 2048)
                    RS = fv(rs_off, 2048)
                    DMA(_f_dma(RC, ropec_d[:, :]), [], K(rc_off, 8192), "rope")
                    DMA(_f_dma(RS, ropes_d[:, :]), [], K(rs_off, 8192), "rope")
                    tix = 0
                    for which in range(2):
                        for c in range(2):
                            wa, wak = load_wblock(l, which * 2 + c)
                            wb_, wbk = load_wblock(l, 18 + which * 2 + c)
                            for j in range(NTL):
                                ba = proj_tile(wa, wak, j)
                                bb = proj_tile(wb_, wbk, j)
                                o1 = OSCR + 2 * SLOT + (tix % 2) * 4096
                                tix += 1
                                t1v, t2v = fv(o1, 512), fv(o1 + 2048, 512)
                                DVE(_f_tt(t1v, bank(ba), RC[:, j * 512:(j + 1) * 512], ALU.mult),
                                    BK(ba) + K(rc_off + j * 2048, 2048), K(o1, 2048))
                                DVE(_f_tt(t2v, bank(bb), RS[:, j * 512:(j + 1) * 512], ALU.mult),
                                    BK(bb) + K(rs_off + j * 2048, 2048), K(o1 + 2048, 2048))
                                if which == 0:
                                    ov, ok = QTV[:, c, j * 512:(j + 1) * 512], QTK(c, j * 512, (j + 1) * 512)
                                else:
                                    ov = KTV[:, c, past + j * 512:past + (j + 1) * 512]
                                    ok = KTK(c, past + j * 512, past + (j + 1) * 512)
                                POOL(_f_tt(ov, t1v, t2v, ALU.add), K(o1, 4096), ok)
                else:
                    for which in range(2):
                        for c in range(2):
                            wa, wak = load_wblock(l, which * 2 + c)
                            for j in range(NTL):
                                ba = proj_tile(wa, wak, j)
                                if which == 0:
                                    ov, ok = QTV[:, c, j * 512:(j + 1) * 512], QTK(c, j * 512, (j + 1) * 512)
                                else:
                                    ov, ok = KTV[:, c, j * 512:(j + 1) * 512], KTK(c, j * 512, (j + 1) * 512)
                                if j % 2 == 0:
                                    ACT(_f_acopy(ov, bank(ba)), BK(ba), ok)
                                else:
                                    DVE(_f_copy(ov, bank(ba)), BK(ba), ok)

                _stage(4 + 10 * l + (50 if is_sample else 0))
                if "memset" not in _KSKIP:
                    DVE(_f_memset(VAV[:, :, :, 64:65], 1.0), [], VAK(0, NKT))
                if is_sample:
                    WV3 = bv(OWV, KC * 256).rearrange("p (k n) -> p k n", k=KC)
                    k_wv = K(OWV, 4096)
                    DMA(_f_dma(WV3, wkvb[l].rearrange("p (k n) -> p k n", k=KC)[:, :, 256:512]),
                        [("dram", "wkv%d" % l)], k_wv, "wv")
                    for kt in range(2):
                        so = OSCR + kt * 2048
                        cks, cvs = fv(so, 256), fv(so + 1024, 256)
                        DMA(_f_dma(cks.rearrange("p (h d) -> p h d", h=4),
                                   ck_d[l, :, kt * 128:(kt + 1) * 128, :].rearrange("h t d -> t h d")),
                            [], K(so, 1024), "cks")
                        DMA(_f_dma(cvs.rearrange("p (h d) -> p h d", h=4),
                                   cv_d[l, :, kt * 128:(kt + 1) * 128, :].rearrange("h t d -> t h d")),
                            [], K(so + 1024, 1024), "cks")
                        for c in range(2):
                            b = nb()
                            PE(_f_tr(bank(b)[:, 0:128], cks[:, c * 128:(c + 1) * 128], IDF),
                               K(so, 1024) + K_IDF, BK(b))
                            ACT(_f_acopy(KTV[:, c, kt * 128:(kt + 1) * 128], bank(b)[:, 0:128]), BK(b),
                                KTK(c, kt * 128, (kt + 1) * 128))
                        DVE(_f_copy(VAV[:, kt, :, 0:64], cvs.rearrange("p (h d) -> p h d", h=4)),
                            K(so + 1024, 1024), VAK(kt, kt + 1))
                    for i in range(NT // 128):
                        b = nb()
                        for kc in range(KC):
                            PE(_f_mm(bank(b)[:, 0:256], HNV[:, kc, i * 128:(i + 1) * 128], WV3[:, kc, :],
                                     kc == 0, kc == KC - 1),
                               HNK(kc, i * 128, (i + 1) * 128) + k_wv, BK(b))
                        src = bank(b)[:, 0:256].rearrange("p (h d) -> p h d", h=4)
                        if i % 2 == 0:
                            ACT(_f_acopy(VAV[:, 2 + i, :, 0:64], src), BK(b), VAK(2 + i, 3 + i))
                        else:
                            DVE(_f_copy(VAV[:, 2 + i, :, 0:64], src), BK(b), VAK(2 + i, 3 + i))
                else:
                    wkv_off = OX + 32768
                    WKV3 = bv(wkv_off, KC * 512).rearrange("p (k n) -> p k n", k=KC)
                    k_wkv = K(wkv_off, 8192)
                    DMA(_f_dma(WKV3, wkvb[l].rearrange("p (k n) -> p k n", k=KC)), [("dram", "wkv%d" % l)],
                        k_wkv, "wv")
                    for i in range(NT // 128):
                        s_, half = i // 2, i % 2
                        b = nb()
                        for kc in range(KC):
                            PE(_f_mm(bank(b), HNV[:, kc, i * 128:(i + 1) * 128], WKV3[:, kc, :],
                                     kc == 0, kc == KC - 1),
                               HNK(kc, i * 128, (i + 1) * 128) + k_wkv, BK(b))
                        so = OX + 40960 + (i % 2) * 2048
                        stg = fv(so, 512)
                        ACT(_f_acopy(stg, bank(b)), BK(b), K(so, 2048))
                        if "vacopy" not in _KSKIP:
                            DVE(_f_copy(VAV[:, i, :, 0:64], bank(b)[:, 256:512].rearrange("p (h d) -> p h d", h=4)),
                                BK(b), VAK(i, i + 1))
                        if "kvout" in _KSKIP:
                            continue
                        DMA(_f_dma(nck_d[s_, l, :, half * 128:(half + 1) * 128, :].rearrange("h t d -> t h d"),
                                   stg[:, 0:256].rearrange("p (h d) -> p h d", h=4)),
                            K(so, 2048), [], "kvs%d" % (i % 2))
                        DMA(_f_dma(ncv_d[s_, l, :, half * 128:(half + 1) * 128, :].rearrange("h t d -> t h d"),
                                   stg[:, 256:512].rearrange("p (h d) -> p h d", h=4)),
                            K(so, 2048), [], "kvs%d" % (i % 2))

                _stage(5 + 10 * l + (50 if is_sample else 0))
                S0, S1, S2, S3, S4 = [slot_f(i, NT) for i in range(5)]
                XCB = bv(OWV, NT) if is_sample else bv(OX + 45056, NT)
                xcb_off = OWV if is_sample else OX + 45056

                def XCBK(t0, t1):
                    return K(xcb_off + t0 * 2, (t1 - t0) * 2)

                def s3d(v, lo, hi, P=None):
                    return v.rearrange("p (s t) -> p s t", s=nseq)[:, :, lo:hi]

                for c in range(2):
                    wv_, wk_ = load_wblock(l, 10 + c)
                    for j in range(NTL):
                        b = proj_tile(wv_, wk_, j)
                        ACT(_f_acopy(S1[:, j * 512:(j + 1) * 512], bank(b)), BK(b), SK(1, j * 512, (j + 1) * 512))
                    wv_, wk_ = load_wblock(l, 8 + c)
                    for j in range(NTL):
                        b = proj_tile(wv_, wk_, j)
                        DVE(_f_tt(S2[:, j * 512:(j + 1) * 512], bank(b), S1[:, j * 512:(j + 1) * 512], ALU.mult),
                            BK(b) + SK(1, j * 512, (j + 1) * 512), SK(2, j * 512, (j + 1) * 512))
                    wv_, wk_ = load_wblock(l, 6 + c)
                    for j in range(NTL):
                        b = proj_tile(wv_, wk_, j)
                        ACT(_f_acopy(S0[:, j * 512:(j + 1) * 512], bank(b)), BK(b), SK(0, j * 512, (j + 1) * 512))
                    cw = lambda tap: spc("convb", (l * 3 + tap) * 2 + c)
                    DVE(_f_ts(S3, S2, cw(1), ALU.mult), SK(2, 0, NT) + K_SP, SK(3, 0, NT))
                    DVE(_f_stt(s3d(S3, 1, T), s3d(S2, 0, T - 1), cw(0), s3d(S3, 1, T), ALU.mult, ALU.add),
                        SK(2, 0, NT) + SK(3, 0, NT) + K_SP, SK(3, 0, NT))
                    DVE(_f_stt(s3d(S3, 0, T - 1), s3d(S2, 1, T), cw(2), s3d(S3, 0, T - 1), ALU.mult, ALU.add),
                        SK(2, 0, NT) + SK(3, 0, NT) + K_SP, SK(3, 0, NT))
                    POOL(_f_tt(Y6V[:, c, :], S0, S3, ALU.mult), SK(0, 0, NT) + SK(3, 0, NT), Y6K(c, 0, NT))

                    wv_, wk_ = load_wblock(l, 12 + c)
                    for j in range(NTL):
                        b = proj_tile(wv_, wk_, j)
                        ACT(_f_acopy(S0[:, j * 512:(j + 1) * 512], bank(b)), BK(b), SK(0, j * 512, (j + 1) * 512))
                    ccw = lambda tap: spc("convc", (l * 4 + tap) * 2 + c)
                    DVE(_f_ts(S1, S0, ccw(2), ALU.mult, spc("convcb", l * 2 + c), ALU.add),
                        SK(0, 0, NT) + K_SP, SK(1, 0, NT))
                    for tap, sh in ((0, 2), (1, 1)):
                        DVE(_f_stt(s3d(S1, sh, T), s3d(S0, 0, T - sh), ccw(tap), s3d(S1, sh, T), ALU.mult, ALU.add),
                            SK(0, 0, NT) + SK(1, 0, NT) + K_SP, SK(1, 0, NT))
                    DVE(_f_stt(s3d(S1, 0, T - 1), s3d(S0, 1, T), ccw(3), s3d(S1, 0, T - 1), ALU.mult, ALU.add),
                        SK(0, 0, NT) + SK(1, 0, NT) + K_SP, SK(1, 0, NT))
                    POOL(_f_copy(XCB, S1), SK(1, 0, NT), XCBK(0, NT))
                    wv_, wk_ = load_wblock(l, 14 + c)
                    for j in range(NTL):
                        b = proj_tile(wv_, wk_, j)
                        ACT(_f_acopy(S0[:, j * 512:(j + 1) * 512], bank(b)), BK(b), SK(0, j * 512, (j + 1) * 512))
                    DVE(_f_tt(S2, S0, S0, ALU.mult), SK(0, 0, NT), SK(2, 0, NT))
                    DVE(_f_ts(S2, S2, 0.044715, ALU.mult, 1.0, ALU.add), SK(2, 0, NT), SK(2, 0, NT))
                    DVE(_f_tt(S2, S2, S0, ALU.mult), SK(2, 0, NT) + SK(0, 0, NT), SK(2, 0, NT))
                    ACT(_f_act(S2, S2, AF.Tanh, scale=GELU_K), SK(2, 0, NT), SK(2, 0, NT))
                    DVE(_f_stt(S0, S2, 1.0, S0, ALU.add, ALU.mult), SK(2, 0, NT) + SK(0, 0, NT), SK(0, 0, NT))
                    for d in range(2):
                        TMP, OUT = (S4, S4) if d == 0 else (S1, S1)
                        ti, oi = (4, 4) if d == 0 else (1, 1)
                        for j in range(NTL):
                            for g_, dst, di in ((0, S2, 2), (1, S3, 3)):
                                b = nb()
                                PE(_f_mm(bank(b), wsm(((d * 2 + g_) * 2 + c)), XCB[:, j * 512:(j + 1) * 512],
                                         True, True), K_WSM + XCBK(j * 512, (j + 1) * 512), BK(b))
                                ACT(_f_act(dst[:, j * 512:(j + 1) * 512], bank(b), AF.Sigmoid,
                                           bias=spc("rgb", ((l * 2 + d) * 2 + g_) * 2 + c)),
                                    BK(b) + K_SP, SK(di, j * 512, (j + 1) * 512))
                        DVE(_f_tt(S3, S3, S1, ALU.mult), SK(3, 0, NT) + SK(1, 0, NT), SK(3, 0, NT))
                        clc = misc(32 + (l * 2 + d) * 2 + c)
                        cl2c = misc(48 + (l * 2 + d) * 2 + c)
                        ACT(_f_act(TMP, S2, AF.Exp, scale=cl2c), SK(2, 0, NT) + K_MISC, SK(ti, 0, NT))
                        ACT(_f_act(S2, S2, AF.Exp, scale=clc), SK(2, 0, NT) + K_MISC, SK(2, 0, NT))
                        ACT(_f_act(TMP, TMP, AF.Sqrt, bias=1.0, scale=-1.0), SK(ti, 0, NT), SK(ti, 0, NT))
                        DVE(_f_tt(S3, S3, TMP, ALU.mult), SK(3, 0, NT) + SK(ti, 0, NT), SK(3, 0, NT))
                        for s_ in range(nseq):
                            init = spc("h0", (l * 2 + d) * 2 + c) if is_sample else 0.0
                            a0, a1 = s_ * T, (s_ + 1) * T
                            if d == 0:
                                oa, aa, ba_ = OUT[:, a0:a1], S2[:, a0:a1], S3[:, a0:a1]
                            else:
                                oa = OUT[:, a0:a1][:, ::-1]
                                aa = S2[:, a0:a1][:, ::-1]
                                ba_ = S3[:, a0:a1][:, ::-1]
                            DVE(_f_scan(oa, aa, ba_, init),
                                SK(2, a0, a1) + SK(3, a0, a1) + K_SP, SK(oi, a0, a1))
                            if not is_sample:
                                col = ((s_ * 2 + l) * 2 + d) * 2 + c
                                pos = a1 - 1 if d == 0 else a0
                                DVE(_f_copy(HSTV[:, col:col + 1], OUT[:, pos:pos + 1]), SK(oi, a0, a1), K_HST)
                    DVE(_f_tt(S4, S4, S1, ALU.add), SK(4, 0, NT) + SK(1, 0, NT), SK(4, 0, NT))
                    DVE(_f_stt(Y6V[:, 2 + c, :], S4, 0.5, S0, ALU.mult, ALU.mult),
                        SK(4, 0, NT) + SK(0, 0, NT), Y6K(2 + c, 0, NT))

                    PW = T + 24
                    P0 = slot_f(0, nseq * PW).rearrange("p (s t) -> p s t", s=nseq)
                    P1 = slot_f(1, nseq * PW).rearrange("p (s t) -> p s t", s=nseq)
                    P2 = slot_f(2, nseq * PW).rearrange("p (s t) -> p s t", s=nseq)
                    P3 = slot_f(3, nseq * PW).rearrange("p (s t) -> p s t", s=nseq)
                    P4 = slot_f(4, nseq * PW).rearrange("p (s t) -> p s t", s=nseq)
                    DVE(_f_memset(P0[:, :, 0:8], 0.0), [], SK(0))
                    DVE(_f_memset(P0[:, :, 8 + T:PW], 0.0), [], SK(0))
                    wv_, wk_ = load_wblock(l, 16 + c)
                    for j in range(NTL):
                        b = proj_tile(wv_, wk_, j)
                        for (s_, ta, tb, c0) in seq_pieces(j):
                            ACT(_f_acopy(P0[:, s_, 8 + ta:8 + tb], bank(b)[:, c0:c0 + (tb - ta)]), BK(b), SK(0))
                    DVE(_f_tt(P1[:, :, 0:PW - 1], P0[:, :, 0:PW - 1], P0[:, :, 1:PW], ALU.add), SK(0), SK(1))
                    if c == 0:
                        DVE(_f_tt(P2[64:128, :, 0:PW - 3], P1[64:128, :, 0:PW - 3], P1[64:128, :, 2:PW - 1], ALU.add),
                            SK(1), SK(2))
                        srcs = [(P1, 1, 7), (P2, 2, 6)]
                        R, ri = P3, 3
                    else:
                        DVE(_f_tt(P2[:, :, 0:PW - 3], P1[:, :, 0:PW - 3], P1[:, :, 2:PW - 1], ALU.add), SK(1), SK(2))
                        DVE(_f_tt(P3[:, :, 0:PW - 7], P2[:, :, 0:PW - 7], P2[:, :, 4:PW - 3], ALU.add), SK(2), SK(3))
                        DVE(_f_tt(P4[64:128, :, 0:PW - 15], P3[64:128, :, 0:PW - 15], P3[64:128, :, 8:PW - 7],
                                  ALU.add), SK(3), SK(4))
                        srcs = [(P3, 3, 4), (P4, 4, 0)]
                        R, ri = P1, 1
                    for hf, (src, si, sh) in enumerate(srcs):
                        pl, ph = hf * 64, hf * 64 + 64
                        pinv = SPV[pl:ph, SP["pinv"] + c:SP["pinv"] + c + 1]
                        DVE(_f_ts(R[pl:ph, :, 0:T], src[pl:ph, :, sh:sh + T], pinv, ALU.mult),
                            SK(si) + K_SP, SK(ri))
                        e0 = SP["pedge"] + c * 32
                        for s_ in range(nseq):
                            DVE(_f_tt(R[pl:ph, s_, 0:16], src[pl:ph, s_, sh:sh + 16], SPV[pl:ph, e0:e0 + 16],
                                      ALU.mult), SK(si) + K_SP, SK(ri))
                            DVE(_f_tt(R[pl:ph, s_, T - 16:T], src[pl:ph, s_, sh + T - 16:sh + T],
                                      SPV[pl:ph, e0 + 16:e0 + 32], ALU.mult), SK(si) + K_SP, SK(ri))
                    DVE(_f_tt(XCB.rearrange("p (s t) -> p s t", s=nseq), R[:, :, 0:T], P0[:, :, 8:8 + T],
                              ALU.subtract), SK(ri) + SK(0), XCBK(0, NT))
                    for j in range(NTL):
                        b = nb()
                        PE(_f_mm(bank(b), wsm(8 + c), XCB[:, j * 512:(j + 1) * 512], True, True),
                           K_WSM + XCBK(j * 512, (j + 1) * 512), BK(b))
                        ACT(_f_act(Y6V[:, 4 + c, j * 512:(j + 1) * 512], bank(b), AF.Identity,
                                   scale=spc("pscale", l * 2 + c)),
                            BK(b) + K_SP, Y6K(4 + c, j * 512, (j + 1) * 512))

                _stage(6 + 10 * l + (50 if is_sample else 0))
                rr["set"] = [0, 1, 2, 3]
                scale = 32.0 ** -0.5
                neglam = misc(l * 16 + 5)
                wsub = SPV[:, SP["wsub"] + l * 64:SP["wsub"] + l * 64 + 64]
                QN = min(512, T)
                NQS = QN // 128
                NQT = T // QN
                PT_off = [OSCR + 1 * SLOT, OSCR + 1 * SLOT + 2048]
                OTS_off = OSCR + 1 * SLOT + 4096
                YT_off = [OSCR + 2 * SLOT, OSCR + 2 * SLOT + 4096]
                OD_off = OSCR + 3 * SLOT
                DVE(_f_memset(bv(OSCR + 4 * SLOT, 4096), 0.0), [], K(OSCR + 4 * SLOT, 8192))
                units = [(s_, qt, h) for s_ in range(nseq) for qt in range(NQT) for h in range(4)]
                steps = [(u, kt) for u in range(len(units)) for kt in range(NK)]

                QM_off = OSCR + 4 * SLOT
                k_qm = K(QM_off, 8192)

                def qm_view(h, m):
                    return bv(QM_off + (h * 2 + m) * 1024, 512)

                def emit_qm(s_, qt):
                    q0 = s_ * T + qt * QN
                    for h in range(4):
                        pb, ch = (h % 2) * 64, h // 2
                        for m in range(2):
                            r0 = pb + 32 * m
                            v = qm_view(h, m)
                            if (h * 2 + m) % 2 == 0:
                                DVE(_f_copy(v[r0:r0 + 32, 0:QN], QTV[r0:r0 + 32, ch, q0:q0 + QN]),
                                    QTK(ch, q0, q0 + QN), K(QM_off + (h * 2 + m) * 1024, 1024))
                            else:
                                ACT(_f_acopy(v[r0:r0 + 32, 0:QN], QTV[r0:r0 + 32, ch, q0:q0 + QN]),
                                    QTK(ch, q0, q0 + QN), K(QM_off + (h * 2 + m) * 1024, 1024))

                def emit_S(ix):
                    u, kt = steps[ix]
                    s_, qt, h = units[u]
                    bi = ix % 2
                    ch = h // 2
                    if kt == 0 and h == 0:
                        emit_qm(s_, qt)
                    kcol = s_ * TK + kt * 128
                    for m in range(2):
                        PE(_f_mm(bank(2 * bi + m)[:, 0:QN], KTV[:, ch, kcol:kcol + 128], qm_view(h, m)[:, 0:QN],
                                 True, True),
                           KTK(ch, kcol, kcol + 128) + K(QM_off + (h * 2 + m) * 1024, 1024), BK(2 * bi + m))

                PT3_off = [OHN, OHN + 2048, OHN + 4096]

                def emit_exp(ix):
                    u, kt = steps[ix]
                    bi = ix % 2
                    Sv = PP[bi][:, :].rearrange("p (m q) -> p m q", m=2)
                    PT = bv(PT3_off[ix % 3], 1024).rearrange("p (m q) -> p m q", m=2)
                    ACT(_f_act(PT[:, :, 0:QN], Sv[:, :, 0:QN], AF.Exp, scale=scale),
                        BK(2 * bi) + BK(2 * bi + 1), K(PT3_off[ix % 3], 2048))

                def emit_pv(ix):
                    u, kt = steps[ix]
                    s_, qt, h = units[u]
                    PT = bv(PT3_off[ix % 3], 1024).rearrange("p (m q) -> p m q", m=2)
                    k_pt = K(PT3_off[ix % 3], 2048)
                    for m in range(2):
                        PE(_f_mm(bank(4 + m)[0:65, 0:QN], VAV[:, s_ * NK + kt, h, :], PT[:, m, 0:QN],
                                 kt == 0, kt == NK - 1),
                           k_pt + VAK(s_ * NK + kt, s_ * NK + kt + 1), BK(4 + m))

                OTS = fv(OTS_off, 1024).rearrange("p (m q) -> p m q", m=2)
                k_ots = K(OTS_off, 4096)

                def emit_post_copy(u):
                    ACT(_f_acopy(OTS[0:65, 0, 0:QN], bank(4)[0:65, 0:QN]), BK(4), k_ots)
                    DVE(_f_copy(OTS[0:65, 1, 0:QN], bank(5)[0:65, 0:QN]), BK(5), k_ots)

                def emit_post_a(u):
                    s_, qt, h = units[u]
                    par = u % 2
                    for m in range(2):
                        for qs in range(NQS):
                            PE(_f_tr(bank(6 + m)[:, qs * 65:qs * 65 + 65], OTS[0:65, m, qs * 128:(qs + 1) * 128],
                                     IDF[0:65, 0:65]), k_ots + K_IDF, BK(6 + m))
                    k_a = K(OCB + C_ATT + par * 256, 256)
                    odb = OD_off + par * 4096
                    k_od = K(odb, 4096)
                    o1v = bank(6)[:, 0:NQS * 65].rearrange("p (q d) -> p q d", d=65)
                    o2v = bank(7)[:, 0:NQS * 65].rearrange("p (q d) -> p q d", d=65)
                    r1, r2, nl2, ss, lnv, rstd = [ATT[:, par * 64 + i * 8:par * 64 + i * 8 + NQS] for i in range(6)]
                    DVE(_f_recip(r1, o1v[:, :, 64]), BK(6), k_a)
                    DVE(_f_recip(r2, o2v[:, :, 64]), BK(7), k_a)
                    DVE(_f_ts(nl2, r2, neglam, ALU.mult), k_a + K_MISC, k_a)
                    for qs in range(NQS):
                        odo = odb + qs * 1024
                        od, t2, junk = fv(odo, 64), fv(odo + 256, 64), fv(odo + 512, 64)
                        DVE(_f_ts(t2, o2v[:, qs, 0:64], nl2[:, qs:qs + 1], ALU.mult), BK(7) + k_a, k_od)
                        DVE(_f_stt(od, o1v[:, qs, 0:64], r1[:, qs:qs + 1], t2, ALU.mult, ALU.add),
                            BK(6) + k_a + k_od, k_od)

                def emit_post_b(u):
                    s_, qt, h = units[u]
                    par = u % 2
                    q0 = s_ * T + qt * QN
                    yo = YT_off[(s_ * NQT + qt) % 2]
                    k_yt = K(yo, 4096)
                    k_a = K(OCB + C_ATT + par * 256, 256)
                    odb = OD_off + par * 4096
                    k_od = K(odb, 4096)
                    r1, r2, nl2, ss, lnv, rstd = [ATT[:, par * 64 + i * 8:par * 64 + i * 8 + NQS] for i in range(6)]
                    for qs in range(NQS):
                        odo = odb + qs * 1024
                        ACT(_f_act(fv(odo + 512, 64), fv(odo, 64), AF.Square, accum_out=ss[:, qs:qs + 1]),
                            k_od, k_od + k_a)
                    ACT(_f_act(lnv, ss, AF.Ln, bias=EPS, scale=1.0 / 64), k_a, k_a)
                    ACT(_f_act(rstd, lnv, AF.Exp, scale=-0.5), k_a, k_a)
                    for qs in range(NQS):
                        od = fv(odb + qs * 1024, 64)
                        YT = fv(yo + qs * 1024, 256)
                        DVE(_f_stt(YT[:, h * 64:(h + 1) * 64], od, rstd[:, qs:qs + 1], wsub, ALU.mult, ALU.mult),
                            k_od + k_a + K_SP, k_yt)
                    if h == 3:
                        for ch in range(2):
                            for qs in range(NQS):
                                YT = fv(yo + qs * 1024, 256)
                                PE(_f_tr(bank(6 + ch)[:, qs * 128:(qs + 1) * 128], YT[:, ch * 128:(ch + 1) * 128], IDF),
                                   k_yt + K_IDF, BK(6 + ch))
                            if ch == 0:
                                ACT(_f_acopy(YAV[:, ch, q0:q0 + QN], bank(6 + ch)[:, 0:QN]), BK(6 + ch),
                                    YAK(ch, q0, q0 + QN))
                            else:
                                DVE(_f_copy(YAV[:, ch, q0:q0 + QN], bank(6 + ch)[:, 0:QN]), BK(6 + ch),
                                    YAK(ch, q0, q0 + QN))

                emit_S(0)
                if len(steps) > 1:
                    emit_S(1)
                todo = []
                DA, DB = 1, (4 if NK >= 8 else 2)
                for ix in range(len(steps)):
                    emit_exp(ix)
                    if ix + 2 < len(steps):
                        emit_S(ix + 2)
                    emit_pv(ix)
                    for it in [t_ for t_ in todo if t_[0] <= ix]:
                        it[1](it[2])
                    todo = [t_ for t_ in todo if t_[0] > ix]
                    u, kt = steps[ix]
                    if kt == NK - 1:
                        emit_post_copy(u)
                        todo.append((ix + DA, emit_post_a, u))
                        todo.append((ix + DB, emit_post_b, u))
                for it in todo:
                    it[1](it[2])

                _stage(7 + 10 * l + (50 if is_sample else 0))
                rr["set"] = [4, 5, 6, 7]
                if (not is_sample) and l == 0 and DEPTH > 1:
                    rr["set"] = [4, 5, 6]
                    setup_layer(1, [OX + 32768 + i * 8192 for i in range(4)], 7)
                WO3 = bv(OSCR + 2 * SLOT, KC * D).rearrange("p (k n) -> p k n", k=KC)
                k_wo = K(OSCR + 2 * SLOT, 16384)
                DMA(_f_dma(WO3, woutb[l].rearrange("p (k n) -> p k n", k=KC)), [("dram", "wout%d" % l)], k_wo, "wo")
                HN2 = bv(OSCR + SLOT, KC * 512).rearrange("p (c t) -> p c t", c=KC)
                HV = bv(OHN, 32 * 512).rearrange("p (h t) -> p h t", h=32)
                W1_off = [OKT, OKT + 8192]
                W2_off = [OKT + 16384, OKT + 20480]
                RT_off = [OWS + 4096, OWS + 6144]
                w1c = {"i": 0}
                w2c = {"i": 0}
                def st5_wout(j):
                    t0, t1 = j * 512, (j + 1) * 512
                    for oc in range(8):
                        b = nb()
                        for kc in range(KC):
                            if kc < 2:
                                rhs, rk = YAV[:, kc, t0:t1], YAK(kc, t0, t1)
                            else:
                                rhs, rk = Y6V[:, kc - 2, t0:t1], Y6K(kc - 2, t0, t1)
                            PE(_f_mm(bank(b), WO3[:, kc, oc * 128:(oc + 1) * 128], rhs, kc == 0, kc == KC - 1),
                               k_wo + rk, BK(b))
                        DVE(_f_stt(XV[:, oc, t0:t1], bank(b), modc(l, 2, oc, jm), XV[:, oc, t0:t1], ALU.mult, ALU.add),
                            BK(b) + K_MOD + XK(oc, t0, t1), XK(oc, t0, t1))

                def hn2_out(c, tt, tk, sh):
                    DVE(_f_ts(HN2[:, c, :], tt, sh, ALU.add), tk + K_MOD, K(OSCR + SLOT + c * 1024, 1024))

                w1pre = {}

                def w1_load(hg):
                    i1 = w1c["i"]
                    w1c["i"] = (i1 + 1) % 2
                    W1v = bv(W1_off[i1], 4 * KC * 128).rearrange("p (h k n) -> p h k n", h=4, k=KC)
                    k_w1 = K(W1_off[i1], 8192)
                    DMA(_f_dma(W1v.rearrange("p h k n -> p h (k n)"),
                               w1b[l, hg * 512:(hg + 1) * 512, :].rearrange("(h p) n -> p h n", p=128)),
                        [("dram", "w1%d" % l)], k_w1, "w1s%d" % i1)
                    return W1v, k_w1

                def st5_mlp1(j):
                    for hg in range(8):
                        if (j, hg) in w1pre:
                            W1v, k_w1 = w1pre.pop((j, hg))
                        else:
                            W1v, k_w1 = w1_load(hg)
                        for hh in range(4):
                            hc = hg * 4 + hh
                            b = nb()
                            for kc in range(KC):
                                PE(_f_mm(bank(b), W1v[:, hh, kc, :], HN2[:, kc, :], kc == 0, kc == KC - 1),
                                   k_w1 + K(OSCR + SLOT + kc * 1024, 1024), BK(b))
                            ro = RT_off[hc % 2]
                            rt = fv(ro, 512)
                            ACT(_f_act(rt, bank(b), AF.Square), BK(b), K(ro, 2048))
                            DVE(_f_stt(HV[:, hc, :], bank(b), 0.0, rt, ALU.is_gt, ALU.mult),
                                BK(b) + K(ro, 2048), K(OHN + hc * 1024, 1024))

                def st5_mlp2_half(j, half):
                    t0, t1 = j * 512, (j + 1) * 512
                    for hg in range(8):
                        i2 = w2c["i"]
                        w2c["i"] = (i2 + 1) % 2
                        W2v = bv(W2_off[i2], 4 * 512).rearrange("p (h n) -> p h n", h=4)
                        k_w2 = K(W2_off[i2], 4096)
                        DMA(_f_dma(W2v, w2b[l, hg * 512:(hg + 1) * 512, half * 512:(half + 1) * 512]
                                   .rearrange("(h p) n -> p h n", p=128)),
                            [("dram", "w2%d" % l)], k_w2, "w2s%d" % i2)
                        for hh in range(4):
                            hc = hg * 4 + hh
                            for o4 in range(4):
                                PE(_f_mm(bank(o4), W2v[:, hh, o4 * 128:(o4 + 1) * 128], HV[:, hc, :],
                                         hc == 0, hc == 31),
                                   k_w2 + K(OHN + hc * 1024, 1024), BK(o4))
                    for o4 in range(4):
                        oc = half * 4 + o4
                        DVE(_f_stt(XV[:, oc, t0:t1], bank(o4), modc(l, 5, oc, jm), XV[:, oc, t0:t1],
                                   ALU.mult, ALU.add),
                            BK(o4) + K_MOD + XK(oc, t0, t1), XK(oc, t0, t1))

                skip_mlp = _KNOMLP and l == min(DEPTH, _KLAYERS) - 1
                for j in range(NTL):
                    st5_wout(j)
                    if skip_mlp:
                        continue
                    pre = (lambda jj=j: st5_mlp2_half(jj - 1, 0)) if j > 0 else None
                    post = (lambda jj=j: st5_mlp2_half(jj - 1, 1)) if j > 0 else None
                    norm_tile(j, lambda c: gc_(l, 1, c, jm), lambda c: modc(l, 3, c, jm), hn2_out, NTMP5,
                              pe_pre=pre, pe_post=post)
                    st5_mlp1(j)
                    if j + 1 < NTL:
                        w1pre[(j + 1, 0)] = w1_load(0)
                        w1pre[(j + 1, 1)] = w1_load(1)
                if not skip_mlp:
                    st5_mlp2_half(NTL - 1, 0)
                    st5_mlp2_half(NTL - 1, 1)

            _stage(30 + (50 if is_sample else 0))
            rr["set"] = [0, 1, 2, 3, 4, 5]
            oc_ = {"i": 0}
            for j in range(NTL):
                t0 = j * 512
                FT = fv(OHN, 8 * 512).rearrange("p (c t) -> p c t", c=8)

                def fin_out(c, tt, tk, sh):
                    POOL(_f_copy(FT[:, c, :], tt), tk, K(OHN + c * 2048, 2048))

                if _KRAW:
                    for c in range(8):
                        DVE(_f_copy(FT[:, c, :], XV[:, c, t0:t0 + 512]), XK(c, t0, t0 + 512), K(OHN + c * 2048, 2048))
                else:
                    norm_tile(j, lambda c: spc("fnw", c), None, fin_out, OSCR)
                for i4 in range(4):
                    oi = oc_["i"]
                    oc_["i"] = (oi + 1) % 2
                    so = OY6 + oi * 4096
                    ost = fv(so, 1024)
                    for half in range(2):
                        b = nb()
                        for q in range(4):
                            c = half * 4 + q
                            PE(_f_tr(bank(b)[:, q * 128:(q + 1) * 128], FT[:, c, i4 * 128:(i4 + 1) * 128], IDF),
                               K(OHN + c * 2048, 2048) + K_IDF, BK(b))
                        if half == 0:
                            ACT(_f_acopy(ost[:, 0:512], bank(b)), BK(b), K(so, 2048))
                        else:
                            DVE(_f_copy(ost[:, 512:1024], bank(b)), BK(b), K(so + 2048, 2048))
                    r0 = t0 + i4 * 128
                    DMA(_f_dma(y_d[r0:r0 + 128, :], ost), K(so, 4096), [], "os%d" % oi)
            if not is_sample:
                b = nb()
                PE(_f_tr(bank(b)[0:32, 0:128], HSTV, IDF), K_HST + K_IDF, BK(b))
                ho = OY6 + 8192
                hsb = fv(ho, 128)
                ACT(_f_acopy(hsb[0:32, :], bank(b)[0:32, 0:128]), BK(b), K(ho, 512))
                DMA(_f_dma(nst_d[:, :], hsb[0:32, :]), K(ho, 512), [], "hst")

        NTMP5 = OSCR + 4 * SLOT
        try:
            _stage(0)
            run_batch(False)
            run_batch(True)
        except _Stop:
            pass
        S.emit(nc, es)
        _CACHE["sched"] = S
    return nc


def _rope_tables():
    f = np.arange(128)
    d = f % 32
    axis = d // 16
    e = d % 16
    i = e % 8
    freq = (10000.0 ** (-(i.astype(np.float64)) / 8.0))
    t = np.arange(2048)
    rows, cols = t // 64, t % 64
    pos = np.where(axis[:, None] == 0, rows[None, :], cols[None, :]).astype(np.float64)
    ang = (pos.astype(np.float32) * freq.astype(np.float32)[:, None]).astype(np.float32)
    c = np.cos(ang).astype(np.float32)
    s = np.sin(ang).astype(np.float32)
    sgn = np.where(e < 8, -1.0, 1.0).astype(np.float32)[:, None]
    return np.ascontiguousarray(c), np.ascontiguousarray(s * sgn)


def _rope_partner_cols():
    f = np.arange(256)
    e = f % 16
    return np.where(e < 8, f + 8, f - 8)


def _cm(v):
    v = np.asarray(v, np.float32)
    return np.ascontiguousarray(v.reshape(-1, 128).T)


def _blk(w):
    n = w.shape[1] // 128
    return np.ascontiguousarray(w.reshape(KC, 128, n, 128).transpose(2, 1, 0, 3).reshape(n * 128, KC * 128))


def _prep_shared(inp):
    f32 = np.float32
    sh = {}
    w_in = np.asarray(inp["w_in"], f32)
    pc = _rope_partner_cols()
    win = []
    wkv = []
    for l in range(DEPTH):
        w = w_in[l]
        ext = np.concatenate([w, w[:, 0:256][:, pc], w[:, 256:512][:, pc]], axis=1)
        win.append(_blk(ext))
        wkv.append(np.ascontiguousarray(
            w[:, 256:768].reshape(KC, 128, 512).transpose(1, 0, 2).reshape(128, KC * 512)))
    sh["win"] = np.stack(win)
    sh["wkv"] = np.stack(wkv)
    rg_w = np.asarray(inp["rg_w"], f32)
    pool_w = np.asarray(inp["pool_w"], f32)
    wsm = np.zeros((DEPTH, 128, 10, 128), f32)
    for l in range(DEPTH):
        for d in range(2):
            for g in range(2):
                for c in range(2):
                    i = (d * 2 + g) * 2 + c
                    for bb in range(2):
                        wsm[l, bb * 64:(bb + 1) * 64, i, bb * 64:(bb + 1) * 64] = rg_w[l, d, g, c * 2 + bb]
        for c in range(2):
            for bb in range(2):
                wsm[l, bb * 64:(bb + 1) * 64, 8 + c, bb * 64:(bb + 1) * 64] = pool_w[l, c * 2 + bb]
    sh["wsm"] = wsm.reshape(DEPTH, 128, 1280)
    w_out = np.asarray(inp["w_out"], f32)
    sh["wout"] = np.ascontiguousarray(
        w_out.reshape(DEPTH, KC, 128, D).transpose(0, 2, 1, 3).reshape(DEPTH, 128, KC * D))
    w1 = np.asarray(inp["w_mlp1"], f32)
    sh["w1"] = np.stack([_blk(w1[l]) for l in range(DEPTH)])
    w2 = np.asarray(inp["w_mlp2"], f32)
    sh["w2"] = np.ascontiguousarray(w2)
    wada = np.asarray(inp["w_ada"], f32)
    sh["wada"] = np.ascontiguousarray(
        wada.reshape(DEPTH, KC, 128, 12, 512).transpose(0, 3, 2, 1, 4))
    rc, rs = _rope_tables()
    sh["ropec"], sh["ropes"] = rc, rs
    sh["ident"] = np.eye(128, dtype=f32)
    return sh


def _sp_pack(inp, core):
    f32 = np.float32
    sp = np.zeros((128, NSP), f32)

    def put(name, arr):
        arr = np.asarray(arr, f32)
        sp[:, SP[name]:SP[name] + arr.shape[1]] = arr

    put("c", _cm(inp["c"][core]))
    put("cctx", _cm(inp["c_ctx"]))
    put("bada", np.concatenate([_cm(inp["b_ada"][l]) for l in range(DEPTH)], axis=1))
    put("normw", np.concatenate([_cm(inp["norm_w"][l, i]) for l in range(DEPTH) for i in range(2)], axis=1))
    put("fnw", _cm(inp["final_norm_w"]))
    put("convb", np.concatenate([_cm(inp["conv_b_w"][l, t]) for l in range(DEPTH) for t in range(3)], axis=1))
    put("convc", np.concatenate([_cm(inp["conv_c_w"][l, t]) for l in range(DEPTH) for t in range(4)], axis=1))
    put("convcb", np.concatenate([_cm(inp["conv_c_b"][l]) for l in range(DEPTH)], axis=1))
    put("rgb", np.concatenate([_cm(inp["rg_b"][l, d, g]) for l in range(DEPTH) for d in range(2)
                               for g in range(2)], axis=1))
    put("rglam", np.concatenate([_cm(inp["rg_lambda"][l, d]) for l in range(DEPTH) for d in range(2)], axis=1))
    put("pscale", np.concatenate([_cm(inp["pool_scale"][l]) for l in range(DEPTH)], axis=1))
    put("h0", np.concatenate([_cm(inp["state_rglru"][core, l, d]) for l in range(DEPTH) for d in range(2)], axis=1))
    wins = (2, 4, 8, 16)
    pinv = np.zeros((128, 2), f32)
    pedge = np.zeros((128, 64), f32)
    for c in range(2):
        for hf in range(2):
            w = wins[2 * c + hf]
            left = w // 2
            right = w - 1 - left
            pinv[hf * 64:(hf + 1) * 64, c] = 1.0 / w
            for i in range(16):
                cnt_first = i + right - max(i - left, 0) + 1
                cnt_last = min(right, 15 - i) + left + 1
                pedge[hf * 64:(hf + 1) * 64, c * 32 + i] = 1.0 / cnt_first
                pedge[hf * 64:(hf + 1) * 64, c * 32 + 16 + i] = 1.0 / cnt_last
    put("pinv", pinv)
    put("pedge", pedge)
    dl = np.asarray(inp["diff_lambda"], f32).reshape(DEPTH * 128)
    put("dl", np.broadcast_to(dl[None, :], (128, DEPTH * 128)))
    ws = np.asarray(inp["subln_w"], f32).reshape(DEPTH * 64)
    put("wsub", np.broadcast_to(ws[None, :], (128, DEPTH * 64)))
    return sp


_CACHE = {}
import os
_KSTOP = int(os.environ.get("KSTOP", "999"))
_MUTE_INIT = True


class _Stop(Exception):
    pass


_DEBUG = os.environ.get("KDEBUG", "") == "1"
_KLAYERS = int(os.environ.get("KLAYERS", "2"))
_KRAW = os.environ.get("KRAW", "") == "1"
_KNOMLP = os.environ.get("KNOMLP", "") == "1"
_MUTE = {"on": False}
_KSKIP = set(os.environ.get("KSKIP", "").split(","))


def _stage(n):
    if n >= _KSTOP:
        if n < 0:
            _MUTE["on"] = True
        else:
            raise _Stop()


def kernel(**inputs):
    f32 = np.float32
    if "nc" not in _CACHE:
        _CACHE["nc"] = build_program()
    nc = _CACHE["nc"]
    sh = _prep_shared(inputs)
    x_prompt = np.asarray(inputs["x_prompt"], f32)
    x_sample = np.asarray(inputs["x_sample"], f32)
    cache_k = np.asarray(inputs["cache_k"], f32)
    cache_v = np.asarray(inputs["cache_v"], f32)
    in_maps = []
    for i in range(8):
        m = dict(sh)
        m["xs"] = np.ascontiguousarray(x_sample[i])
        m["xp"] = np.ascontiguousarray(x_prompt[4 * i:4 * i + 4].reshape(1024, D))
        m["ck"] = np.ascontiguousarray(cache_k[i])
        m["cv"] = np.ascontiguousarray(cache_v[i])
        m["sp"] = _sp_pack(inputs, i)
        in_maps.append(m)
    res = run_bass_kernel_spmd(nc, in_maps, core_ids=list(range(8)))
    outs = res.results
    y_prompt = np.concatenate([np.asarray(o["yp"], f32).reshape(4, 256, D) for o in outs], axis=0)
    y_sample = np.stack([np.asarray(o["ys"], f32) for o in outs], axis=0)
    nck = np.concatenate([np.asarray(o["nck"], f32) for o in outs], axis=0)
    ncv = np.concatenate([np.asarray(o["ncv"], f32) for o in outs], axis=0)
    nst = np.concatenate([np.asarray(o["nst"], f32).reshape(4, DEPTH, 2, 256) for o in outs], axis=0)
    return (y_prompt, y_sample, nck, ncv, nst)
```

```python
import math
from contextlib import ExitStack

import numpy as np
import concourse.bass as bass
import concourse.mybir as mybir
from concourse.bass_utils import run_bass_kernel_spmd

F32 = mybir.dt.float32
BF16 = mybir.dt.bfloat16
AF = mybir.ActivationFunctionType
ALU = mybir.AluOpType

D = 1024
KC = 8
DEPTH = 2
EPS = 1e-6
NB_IN = 22
GELU_K = math.sqrt(2.0 / math.pi)

OX = 0
OHN = 65536
OY6 = 98304
OKT = 122880
OVA = 132096
OQT = 141568
OSCR = 149760
SLOT = 8448
OWS = 192000
OWV = 198144
OCB = 202240
ARENA_BYTES = 212736
C_IDF = 0
C_ONES = 512
C_SP = 768
C_MOD = 3584
C_G = 4352
C_SC = 4608
C_MISC = 4864
C_WSM = 5376
C_HST = 7936
C_ATT = 8192
NSP_MAX = 704

SP = {}
_o = 0
for _n, _w in [("c", 8), ("cctx", 8), ("bada", 96), ("normw", 32), ("fnw", 8), ("convb", 12), ("convc", 16),
               ("convcb", 4), ("rgb", 16), ("rglam", 8), ("pscale", 4), ("h0", 8), ("pinv", 2), ("pedge", 64),
               ("dl", 256), ("wsub", 128)]:
    SP[_n] = _o
    _o += _w
NSP = _o
assert NSP <= NSP_MAX


class _Op:
    __slots__ = ("eng", "fn", "deps", "is_dma", "dsem", "ticket", "signal", "dmawaits")

    def __init__(self, eng, fn, is_dma, dsem):
        self.eng = eng
        self.fn = fn
        self.deps = []
        self.is_dma = is_dma
        self.dsem = dsem
        self.ticket = 0
        self.signal = False
        self.dmawaits = {}


class Sched:
    def __init__(self):
        self.ops = []
        self.lastw = {}
        self.readers = {}
        self.dcount = {}
        self.dbg = []

    def add(self, eng, fn, reads=(), writes=(), dsem=None):
        if _MUTE["on"]:
            return -1
        idx = len(self.ops)
        is_dma = dsem is not None
        pr = [k for k in reads if k[0] == "ps"]
        if pr:
            writes = list(writes) + pr
            reads = [k for k in reads if k[0] != "ps"]
        op = _Op(eng, fn, is_dma, dsem)
        if _DEBUG:
            import sys as _sys
            fr = _sys._getframe(1)
            while fr.f_code.co_name in ("<lambda>", "add", "PE", "ACT", "DVE", "POOL", "DMA"):
                fr = fr.f_back
            self.dbg.append((fr.f_lineno, list(reads), list(writes)))
        need = {}
        for k in reads:
            w = self.lastw.get(k)
            if w is not None:
                need[w] = need.get(w, False) or True
        for k in writes:
            w = self.lastw.get(k)
            if w is not None:
                need.setdefault(w, False)
            rd = self.readers.get(k)
            if rd:
                for r in rd.values():
                    need.setdefault(r, False)
        for d, raw in need.items():
            od = self.ops[d]
            if od.is_dma:
                op.dmawaits[od.dsem] = 16 * self.dcount[od.dsem]
            elif od.eng != eng or is_dma:
                od.signal = True
                op.deps.append(d)
            elif raw and eng in ("act", "dve", "pool"):
                od.signal = True
                op.deps.append(d)
        if is_dma:
            self.dcount[dsem] = self.dcount.get(dsem, 0) + 1
            op.ticket = 16 * self.dcount[dsem]
        for k in writes:
            self.lastw[k] = idx
            self.readers[k] = {}
        rkey = ("dma", dsem) if is_dma else eng
        for k in reads:
            self.readers.setdefault(k, {})[rkey] = idx
        self.ops.append(op)
        return idx

    def emit(self, nc, es):
        engs = ["pe", "act", "dve", "pool"]
        cnt = {e: 0 for e in engs}
        for op in self.ops:
            if not op.is_dma and op.signal:
                cnt[op.eng] += 1
                op.ticket = cnt[op.eng]
        for e in engs:
            assert cnt[e] < 60000, (e, cnt[e])
        for s, c in self.dcount.items():
            assert 16 * c < 60000, (s, c)
        sem = {e: es.enter_context(nc.semaphore("s_" + e)) for e in engs}
        dsem = {s: es.enter_context(nc.semaphore("d_" + s)) for s in self.dcount}
        block = es.enter_context(nc.Block())
        ops = self.ops
        final = dict((s, 16 * c) for s, c in self.dcount.items())

        def stream(ename):
            def run(e):
                waited = {}
                for op in ops:
                    if op.eng != ename:
                        continue
                    for d in op.deps:
                        od = ops[d]
                        key = ("e", od.eng)
                        if waited.get(key, 0) < od.ticket:
                            e.wait_ge(sem[od.eng], od.ticket)
                            waited[key] = od.ticket
                    for s, v in op.dmawaits.items():
                        key = ("d", s)
                        if waited.get(key, 0) < v:
                            e.wait_ge(dsem[s], v)
                            waited[key] = v
                    ins = op.fn(e)
                    if op.is_dma:
                        ins.then_inc(dsem[op.dsem], 16)
                    elif op.signal:
                        ins.then_inc(sem[op.eng], 1)
                if ename == "sp":
                    for s, v in final.items():
                        e.wait_ge(dsem[s], v)
            return run

        block.tensor(stream("pe"))
        block.scalar(stream("act"))
        block.vector(stream("dve"))
        block.gpsimd(stream("pool"))
        block.sync(stream("sp"))


def _f_mm(out, lhsT, rhs, start, stop, **kw):
    return lambda e: e.matmul(out=out, lhsT=lhsT, rhs=rhs, start=start, stop=stop, **kw)


def _f_tr(out, in_, ident):
    return lambda e: e.transpose(out=out, in_=in_, identity=ident)


def _f_act(out, in_, func, bias=None, scale=None, accum_out=None):
    kw = {}
    if bias is not None:
        kw["bias"] = bias
    if scale is not None:
        kw["scale"] = scale
    if accum_out is not None:
        kw["accum_out"] = accum_out
    return lambda e: e.activation(out=out, in_=in_, func=func, **kw)


def _f_copy(out, in_):
    return lambda e: e.tensor_copy(out=out, in_=in_)


def _f_acopy(out, in_):
    return lambda e: e.copy(out=out, in_=in_)


def _f_tt(out, in0, in1, op):
    return lambda e: e.tensor_tensor(out=out, in0=in0, in1=in1, op=op)


def _f_ts(out, in0, s1, op0, s2=None, op1=None):
    if op1 is None:
        return lambda e: e.tensor_scalar(out=out, in0=in0, scalar1=s1, scalar2=None, op0=op0)
    return lambda e: e.tensor_scalar(out=out, in0=in0, scalar1=s1, scalar2=s2, op0=op0, op1=op1)


def _f_stt(out, in0, scalar, in1, op0, op1):
    return lambda e: e.scalar_tensor_tensor(out=out, in0=in0, scalar=scalar, in1=in1, op0=op0, op1=op1)


def _f_scan(out, d0, d1, init):
    return lambda e: e.tensor_tensor_scan(out=out, data0=d0, data1=d1, initial=init, op0=ALU.mult, op1=ALU.add)


def _f_dma(out, in_):
    return lambda e: e.dma_start(out=out, in_=in_)


def _f_memset(ap, val):
    return lambda e: e.memset(ap, val)


def _f_recip(out, in_):
    return lambda e: e.reciprocal(out=out, in_=in_)


def build_program():
    nc = bass.Bass("TRN2", target_bir_lowering=False)
    S = Sched()

    def din(name, shape, dt=F32):
        return nc.dram_tensor(name, list(shape), dt, kind="ExternalInput").ap()

    def dout(name, shape):
        return nc.dram_tensor(name, list(shape), F32, kind="ExternalOutput").ap()

    def dint(name, shape):
        return nc.dram_tensor(name, list(shape), BF16, kind="Internal").ap()

    xs_d = din("xs", [2048, D])
    xp_d = din("xp", [1024, D])
    ck_d = din("ck", [DEPTH, 4, 256, 64])
    cv_d = din("cv", [DEPTH, 4, 256, 64])
    sp_d = din("sp", [128, NSP])
    ropec_d = din("ropec", [128, 2048])
    ropes_d = din("ropes", [128, 2048])
    ident_d = din("ident", [128, 128])
    wada_d = din("wada", [DEPTH, 12, 128, KC, 512])
    win_d = din("win", [DEPTH, NB_IN * 128, KC * 128])
    wkv_d = din("wkv", [DEPTH, 128, KC * 512])
    wsm_d = din("wsm", [DEPTH, 128, 10 * 128])
    wout_d = din("wout", [DEPTH, 128, KC * D])
    w1_d = din("w1", [DEPTH, 32 * 128, KC * 128])
    w2_d = din("w2", [DEPTH, 32 * 128, D])
    winb = dint("winb", [DEPTH, NB_IN * 128, KC * 128])
    wkvb = dint("wkvb", [DEPTH, 128, KC * 512])
    wsmb = dint("wsmb", [DEPTH, 128, 10 * 128])
    woutb = dint("woutb", [DEPTH, 128, KC * D])
    w1b = dint("w1b", [DEPTH, 32 * 128, KC * 128])
    w2b = dint("w2b", [DEPTH, 32 * 128, D])
    yp_d = dout("yp", [1024, D])
    ys_d = dout("ys", [2048, D])
    nck_d = dout("nck", [4, DEPTH, 4, 256, 64])
    ncv_d = dout("ncv", [4, DEPTH, 4, 256, 64])
    nst_d = dout("nst", [32, 128])

    es = ExitStack()
    with es:
        A = es.enter_context(nc.sbuf_tensor("arena", [128, ARENA_BYTES // 4], F32))
        PP = [es.enter_context(nc.psum_tensor("pp%d" % i, [128, 1024], F32)) for i in range(4)]

        def fv(off, n):
            assert off % 4 == 0
            return A[:, off // 4: off // 4 + n]

        def bv(off, n):
            assert off % 4 == 0 and n % 2 == 0
            return A[:, off // 4: off // 4 + n // 2].bitcast(BF16)

        def K(off, nbytes):
            return [("A", g) for g in range(off // 256, (off + nbytes - 1) // 256 + 1)]

        def bank(b):
            return PP[b // 2][:, (b % 2) * 512:(b % 2) * 512 + 512]

        def BK(b):
            return [("ps", b)]

        rr = {"g": 0, "set": [0, 1, 2, 3, 4, 5]}

        def nb():
            st_ = rr["set"]
            rr["g"] = (rr["g"] + 1) % len(st_)
            return st_[rr["g"]]

        PE = lambda fn, r, w: S.add("pe", fn, r, w)
        ACT = lambda fn, r, w: S.add("act", fn, r, w)
        DVE = lambda fn, r, w: S.add("dve", fn, r, w)
        POOL = lambda fn, r, w: S.add("dve", fn, r, w)
        DMA = lambda fn, r, w, ds, q="sp": S.add(q, fn, r, w, dsem=ds)

        IDF = fv(OCB + C_IDF, 128)
        K_IDF = K(OCB + C_IDF, 512)
        ONESB = bv(OCB + C_ONES, 128)
        K_ONES = K(OCB + C_ONES, 256)
        SPV = fv(OCB + C_SP, NSP)
        K_SP = K(OCB + C_SP, NSP * 4)
        MODV = fv(OCB + C_MOD, 192)
        K_MOD = K(OCB + C_MOD, 768)
        GV = fv(OCB + C_G, 64)
        K_G = K(OCB + C_G, 256)
        SCV = fv(OCB + C_SC, 16)
        K_SC = K(OCB + C_SC, 64)
        MISC = fv(OCB + C_MISC, 128)
        K_MISC = K(OCB + C_MISC, 512)
        WSMV = bv(OCB + C_WSM, 1280)
        K_WSM = K(OCB + C_WSM, 2560)
        HSTV = fv(OCB + C_HST, 32)
        K_HST = K(OCB + C_HST, 128)
        ATT = fv(OCB + C_ATT, 128)
        K_ATT = K(OCB + C_ATT, 512)

        def spc(name, i):
            o = SP[name] + i
            return SPV[:, o:o + 1]

        def modc(l, m, c, j):
            o = (l * 48 + m * 8 + c) * 2 + j
            return MODV[:, o:o + 1]

        def gc_(l, n, c, j):
            o = ((l * 2 + n) * 8 + c) * 2 + j
            return GV[:, o:o + 1]

        def misc(i):
            return MISC[:, i:i + 1]

        cch = {"i": 0}

        def cast(src, dst, rows, cols, name):
            per = max(1, (4 << 20) // (cols * 4))
            r0 = 0
            while r0 < rows:
                r1 = min(rows, r0 + per)
                cch["i"] += 1
                DMA(_f_dma(dst[r0:r1, :], src[r0:r1, :]), [], [("dram", name), ("castchain", cch["i"] % 2)],
                    "c_" + name, q="pool")
                r0 = r1

        def cast_layer(l):
            cast(win_d[l], winb[l], NB_IN * 128, KC * 128, "win%d" % l)
            cast(wkv_d[l].rearrange("p (a b) -> (p a) b", a=4), wkvb[l].rearrange("p (a b) -> (p a) b", a=4),
                 512, 1024, "wkv%d" % l)
            cast(wsm_d[l], wsmb[l], 128, 1280, "wsm%d" % l)
            cast(wout_d[l].rearrange("p (a b) -> (p a) b", a=8), woutb[l].rearrange("p (a b) -> (p a) b", a=8),
                 1024, 1024, "wout%d" % l)
            cast(w1_d[l], w1b[l], 32 * 128, KC * 128, "w1%d" % l)
            cast(w2_d[l], w2b[l], 32 * 128, D, "w2%d" % l)

        _stage(-5)
        DMA(_f_dma(IDF, ident_d[:, :]), [], K_IDF, "cst")
        DMA(_f_dma(SPV, sp_d[:, :]), [], K_SP, "cst")
        DVE(_f_memset(ONESB, 1.0), [], K_ONES)

        _stage(-4)
        SCB = bv(OCB + C_SC + 64, 16)
        K_SCB = K(OCB + C_SC + 64, 32)
        scb3 = SCB.rearrange("p (k j) -> p k j", j=2)
        ACT(_f_act(scb3[:, :, 0], SPV[:, SP["cctx"]:SP["cctx"] + 8], AF.Silu), K_SP, K_SCB)
        ACT(_f_act(scb3[:, :, 1], SPV[:, SP["c"]:SP["c"] + 8], AF.Silu), K_SP, K_SCB)
        modv3 = MODV.rearrange("p (a j) -> p a j", j=2)
        gv4 = GV.rearrange("p (a c j) -> p a c j", c=8, j=2)

        def setup_layer(l, st_offs, mb):
            for blk in range(12):
                off = st_offs[blk % len(st_offs)]
                wst3 = bv(off, KC * 512).rearrange("p (k n) -> p k n", k=KC)
                DMA(_f_dma(wst3, wada_d[l, blk]), [], K(off, 8192), "wada%d" % (blk % len(st_offs)), q="pool")
                for cb in range(4):
                    col = (blk * 4 + cb) * 2
                    for kc in range(KC):
                        PE(_f_mm(bank(mb)[:, col:col + 2], wst3[:, kc, cb * 128:(cb + 1) * 128],
                                 scb3[:, kc, :], kc == 0, kc == KC - 1),
                           K(off, 8192) + K_SCB, BK(mb))
            modp3 = bank(mb)[:, 0:96].rearrange("p (a j) -> p a j", j=2)
            for j in range(2):
                DVE(_f_tt(modv3[:, l * 48:(l + 1) * 48, j], modp3[:, :, j],
                          SPV[:, SP["bada"] + l * 48:SP["bada"] + l * 48 + 48], ALU.add),
                    BK(mb) + K_SP, K_MOD)
            for n in range(2):
                m = 1 if n == 0 else 4
                for j in range(2):
                    a = l * 2 + n
                    DVE(_f_stt(gv4[:, a, :, j], modv3[:, l * 48 + m * 8:l * 48 + m * 8 + 8, j], 1.0,
                               SPV[:, SP["normw"] + a * 8:SP["normw"] + a * 8 + 8], ALU.add, ALU.mult),
                        K_MOD + K_SP, K_G)
            lam_init = 0.8 - 0.6 * math.exp(-0.3 * l)
            base = l * 16
            dl = SPV[:, SP["dl"] + l * 128:SP["dl"] + l * 128 + 128]
            tmp = ATT[:, 64:128]
            k_t = K(OCB + C_ATT + 256, 256)
            for t_ in range(2):
                DVE(_f_tt(tmp[:, t_ * 32:t_ * 32 + 32], dl[:, t_ * 64:t_ * 64 + 32], dl[:, t_ * 64 + 32:t_ * 64 + 64],
                          ALU.mult), K_SP, k_t)
                DVE(lambda e, o=misc(base + t_), i_=tmp[:, t_ * 32:t_ * 32 + 32]:
                    e.tensor_reduce(out=o, in_=i_, axis=mybir.AxisListType.X, op=ALU.add), k_t, K_MISC)
                ACT(_f_act(misc(base + 2 + t_), misc(base + t_), AF.Exp), K_MISC, K_MISC)
            DVE(_f_stt(misc(base + 4), misc(base + 2), lam_init, misc(base + 3), ALU.add, ALU.subtract),
                K_MISC, K_MISC)
            DVE(_f_ts(misc(base + 5), misc(base + 4), -1.0, ALU.mult), K_MISC, K_MISC)
            ws_ = SPV[:, SP["wsub"] + l * 64:SP["wsub"] + l * 64 + 64]
            DVE(_f_ts(ws_, ws_, 1.0 - lam_init, ALU.mult), K_SP, K_SP)
            lamv = SPV[:, SP["rglam"] + l * 4:SP["rglam"] + l * 4 + 4]
            clv = MISC[:, 32 + l * 4:32 + l * 4 + 4]
            cl2v = MISC[:, 48 + l * 4:48 + l * 4 + 4]
            ACT(_f_act(clv, lamv, AF.Exp, scale=-1.0), K_SP, K_MISC)
            ACT(_f_act(clv, clv, AF.Ln, bias=1.0), K_MISC, K_MISC)
            DVE(_f_ts(cl2v, clv, -16.0, ALU.mult), K_MISC, K_MISC)
            DVE(_f_ts(clv, clv, -8.0, ALU.mult), K_MISC, K_MISC)

        _stage(-3)
        setup_layer(0, [OSCR + i * SLOT for i in range(4)], 7)
        cast_layer(0)
        cast_layer(1)
        _stage(-1)

        def run_batch(is_sample):
            nseq = 1 if is_sample else 4
            T = 2048 if is_sample else 256
            NT = nseq * T
            NTL = NT // 512
            past = 256 if is_sample else 0
            jm = 1 if is_sample else 0
            x_d = xs_d if is_sample else xp_d
            y_d = ys_d if is_sample else yp_d
            TK = past + T
            NK = TK // 128
            KTW = nseq * TK
            tag = "s" if is_sample else "p"

            XV = fv(OX, 8 * NT).rearrange("p (c t) -> p c t", c=8)
            HNV = bv(OHN, 8 * NT).rearrange("p (c t) -> p c t", c=8)
            Y6V = bv(OY6, 6 * NT).rearrange("p (c t) -> p c t", c=6)
            KTV = bv(OKT, 2 * KTW).rearrange("p (c t) -> p c t", c=2)
            NKT = nseq * NK
            VAF = bv(OVA, NKT * 260)
            VAV = VAF.rearrange("p (k h d) -> p k h d", h=4, d=65)
            QTV = bv(OQT, 2 * NT).rearrange("p (c t) -> p c t", c=2)
            YAV = bv(OSCR, 2 * NT).rearrange("p (c t) -> p c t", c=2)

            def XK(c, t0, t1):
                return K(OX + (c * NT + t0) * 4, (t1 - t0) * 4)

            def HNK(c, t0, t1):
                return K(OHN + (c * NT + t0) * 2, (t1 - t0) * 2)

            def Y6K(c, t0, t1):
                return K(OY6 + (c * NT + t0) * 2, (t1 - t0) * 2)

            def KTK(c, t0, t1):
                return K(OKT + (c * KTW + t0) * 2, (t1 - t0) * 2)

            def VAK(k0, k1):
                return K(OVA + k0 * 520, (k1 - k0) * 520)

            def QTK(c, t0, t1):
                return K(OQT + (c * NT + t0) * 2, (t1 - t0) * 2)

            def YAK(c, t0, t1):
                return K(OSCR + (c * NT + t0) * 2, (t1 - t0) * 2)

            def slot_f(i, n=None):
                return fv(OSCR + i * SLOT, SLOT // 4 if n is None else n)

            def SK(i, c0=0, c1=None, es_=4):
                if c1 is None:
                    return K(OSCR + i * SLOT, SLOT)
                return K(OSCR + i * SLOT + c0 * es_, (c1 - c0) * es_)

            _stage(1 + (50 if is_sample else 0))
            for i in range(NT // 128):
                soff = OSCR + 4 * SLOT + (i % 2) * 4096
                st = fv(soff, 1024)
                DMA(_f_dma(st, x_d[i * 128:(i + 1) * 128, :]), [], K(soff, 4096), "xs%d" % (i % 2))
                for half in range(2):
                    b = nb()
                    for q in range(4):
                        c = half * 4 + q
                        PE(_f_tr(bank(b)[:, q * 128:(q + 1) * 128], st[:, c * 128:(c + 1) * 128], IDF),
                           K(soff, 4096) + K_IDF, BK(b))
                    outv = XV[:, half * 4:half * 4 + 4, i * 128:(i + 1) * 128]
                    inv = bank(b).rearrange("p (a t) -> p a t", a=4)
                    wk = []
                    for c in range(half * 4, half * 4 + 4):
                        wk += XK(c, i * 128, (i + 1) * 128)
                    if (2 * i + half) % 2 == 0:
                        ACT(_f_acopy(outv, inv), BK(b), wk)
                    else:
                        DVE(_f_copy(outv, inv), BK(b), wk)

            def norm_tile(j, gfn, shfn, out_fn, tmp_off, pe_pre=None, pe_post=None):
                t0, t1 = j * 512, (j + 1) * 512
                sq = bv(tmp_off, 8 * 512).rearrange("p (c t) -> p c t", c=8)
                k_sq = K(tmp_off, 8192)
                xk = []
                for c in range(8):
                    xk += XK(c, t0, t1)
                ACT(_f_act(sq, XV[:, :, t0:t1], AF.Square), xk, k_sq)
                if pe_pre is not None:
                    pe_pre()
                b = nb()
                for c in range(8):
                    PE(_f_mm(bank(b), ONESB, sq[:, c, :], c == 0, c == 7), K_ONES + k_sq, BK(b))
                if pe_post is not None:
                    pe_post()
                ln = fv(tmp_off + 8192, 512)
                k_ln = K(tmp_off + 8192, 2048)
                rs = fv(tmp_off + 10240, 512)
                k_rs = K(tmp_off + 10240, 2048)
                ACT(_f_act(ln, bank(b), AF.Ln, bias=EPS, scale=1.0 / D), BK(b), k_ln)
                ACT(_f_act(rs, ln, AF.Exp, scale=-0.5), k_ln, k_rs)
                for c in range(8):
                    to = tmp_off + 12288 + (c % 2) * 2048
                    tt = fv(to, 512)
                    DVE(_f_stt(tt, XV[:, c, t0:t1], gfn(c), rs, ALU.mult, ALU.mult),
                        XK(c, t0, t1) + k_rs + K_G + K_SP, K(to, 2048))
                    out_fn(c, tt, K(to, 2048), shfn(c) if shfn is not None else None)

            wsr = {"i": 0}

            def load_wblock(l, blk):
                i = wsr["i"]
                wsr["i"] = (i + 1) % 3
                off = OWS + i * 2048
                v = bv(off, KC * 128).rearrange("p (k n) -> p k n", k=KC)
                DMA(_f_dma(v, winb[l, blk * 128:(blk + 1) * 128, :].rearrange("p (k n) -> p k n", k=KC)),
                    [("dram", "win%d" % l)], K(off, 2048), "ws%d" % i)
                return v, K(off, 2048)

            def proj_tile(wv, wk, j):
                b = nb()
                for kc in range(KC):
                    PE(_f_mm(bank(b), wv[:, kc, :], HNV[:, kc, j * 512:(j + 1) * 512], kc == 0, kc == KC - 1),
                       wk + HNK(kc, j * 512, (j + 1) * 512), BK(b))
                return b

            def seq_pieces(j):
                out = []
                g0 = j * 512
                while g0 < (j + 1) * 512:
                    s = g0 // T
                    g1 = min((s + 1) * T, (j + 1) * 512)
                    out.append((s, g0 - s * T, g1 - s * T, g0 - j * 512))
                    g0 = g1
                return out

            for l in range(min(DEPTH, _KLAYERS)):
                rr["set"] = [0, 1, 2, 3, 4, 5]
                lam_init = 0.8 - 0.6 * math.exp(-0.3 * l)
                DMA(_f_dma(WSMV, wsmb[l]), [("dram", "wsm%d" % l)], K_WSM, "wsm")

                def wsm(i):
                    return WSMV[:, i * 128:(i + 1) * 128]

                _stage(2 + 10 * l + (50 if is_sample else 0))
                def hn_out(j):
                    def f(c, tt, tk, sh):
                        DVE(_f_ts(HNV[:, c, j * 512:(j + 1) * 512], tt, sh, ALU.add), tk + K_MOD,
                            HNK(c, j * 512, (j + 1) * 512))
                    return f

                for j in range(NTL):
                    norm_tile(j, lambda c: gc_(l, 0, c, jm), lambda c: modc(l, 0, c, jm), hn_out(j), OSCR)

                _stage(3 + 10 * l + (50 if is_sample else 0))
                if is_sample:
                    rc_off, rs_off = OSCR + 3 * SLOT, OSCR + 4 * SLOT
                    RC = fv(rc_off, 2048)
                    RS = fv(rs_off, 2048)
                    DMA(_f_dma(RC, ropec_d[:, :]), [], K(rc_off, 8192), "rope")
                    DMA(_f_dma(RS, ropes_d[:, :]), [], K(rs_off, 8192), "rope")
                    tix = 0
                    for which in range(2):
                        for c in range(2):
                            wa, wak = load_wblock(l, which * 2 + c)
                            wb_, wbk = load_wblock(l, 18 + which * 2 + c)
                            for j in range(NTL):
                                ba = proj_tile(wa, wak, j)
                                bb = proj_tile(wb_, wbk, j)
                                o1 = OSCR + 2 * SLOT + (tix % 2) * 4096
                                tix += 1
                                t1v, t2v = fv(o1, 512), fv(o1 + 2048, 512)
                                DVE(_f_tt(t1v, bank(ba), RC[:, j * 512:(j + 1) * 512], ALU.mult),
                                    BK(ba) + K(rc_off + j * 2048, 2048), K(o1, 2048))
                                DVE(_f_tt(t2v, bank(bb), RS[:, j * 512:(j + 1) * 512], ALU.mult),
                                    BK(bb) + K(rs_off + j * 2048, 2048), K(o1 + 2048, 2048))
                                if which == 0:
                                    ov, ok = QTV[:, c, j * 512:(j + 1) * 512], QTK(c, j * 512, (j + 1) * 512)
                                else:
                                    ov = KTV[:, c, past + j * 512:past + (j + 1) * 512]
                                    ok = KTK(c, past + j * 512, past + (j + 1) * 512)
                                POOL(_f_tt(ov, t1v, t2v, ALU.add), K(o1, 4096), ok)
                else:
                    for which in range(2):
                        for c in range(2):
                            wa, wak = load_wblock(l, which * 2 + c)
                            for j in range(NTL):
                                ba = proj_tile(wa, wak, j)
                                if which == 0:
                                    ov, ok = QTV[:, c, j * 512:(j + 1) * 512], QTK(c, j * 512, (j + 1) * 512)
                                else:
                                    ov, ok = KTV[:, c, j * 512:(j + 1) * 512], KTK(c, j * 512, (j + 1) * 512)
                                if j % 2 == 0:
                                    ACT(_f_acopy(ov, bank(ba)), BK(ba), ok)
                                else:
                                    DVE(_f_copy(ov, bank(ba)), BK(ba), ok)

                _stage(4 + 10 * l + (50 if is_sample else 0))
                if "memset" not in _KSKIP:
                    DVE(_f_memset(VAV[:, :, :, 64:65], 1.0), [], VAK(0, NKT))
                if is_sample:
                    WV3 = bv(OWV, KC * 256).rearrange("p (k n) -> p k n", k=KC)
                    k_wv = K(OWV, 4096)
                    DMA(_f_dma(WV3, wkvb[l].rearrange("p (k n) -> p k n", k=KC)[:, :, 256:512]),
                        [("dram", "wkv%d" % l)], k_wv, "wv")
                    for kt in range(2):
                        so = OSCR + kt * 2048
                        cks, cvs = fv(so, 256), fv(so + 1024, 256)
                        DMA(_f_dma(cks.rearrange("p (h d) -> p h d", h=4),
                                   ck_d[l, :, kt * 128:(kt + 1) * 128, :].rearrange("h t d -> t h d")),
                            [], K(so, 1024), "cks")
                        DMA(_f_dma(cvs.rearrange("p (h d) -> p h d", h=4),
                                   cv_d[l, :, kt * 128:(kt + 1) * 128, :].rearrange("h t d -> t h d")),
                            [], K(so + 1024, 1024), "cks")
                        for c in range(2):
                            b = nb()
                            PE(_f_tr(bank(b)[:, 0:128], cks[:, c * 128:(c + 1) * 128], IDF),
                               K(so, 1024) + K_IDF, BK(b))
                            ACT(_f_acopy(KTV[:, c, kt * 128:(kt + 1) * 128], bank(b)[:, 0:128]), BK(b),
                                KTK(c, kt * 128, (kt + 1) * 128))
                        DVE(_f_copy(VAV[:, kt, :, 0:64], cvs.rearrange("p (h d) -> p h d", h=4)),
                            K(so + 1024, 1024), VAK(kt, kt + 1))
                    for i in range(NT // 128):
                        b = nb()
                        for kc in range(KC):
                            PE(_f_mm(bank(b)[:, 0:256], HNV[:, kc, i * 128:(i + 1) * 128], WV3[:, kc, :],
                                     kc == 0, kc == KC - 1),
                               HNK(kc, i * 128, (i + 1) * 128) + k_wv, BK(b))
                        src = bank(b)[:, 0:256].rearrange("p (h d) -> p h d", h=4)
                        if i % 2 == 0:
                            ACT(_f_acopy(VAV[:, 2 + i, :, 0:64], src), BK(b), VAK(2 + i, 3 + i))
                        else:
                            DVE(_f_copy(VAV[:, 2 + i, :, 0:64], src), BK(b), VAK(2 + i, 3 + i))
                else:
                    wkv_off = OX + 32768
                    WKV3 = bv(wkv_off, KC * 512).rearrange("p (k n) -> p k n", k=KC)
                    k_wkv = K(wkv_off, 8192)
                    DMA(_f_dma(WKV3, wkvb[l].rearrange("p (k n) -> p k n", k=KC)), [("dram", "wkv%d" % l)],
                        k_wkv, "wv")
                    for i in range(NT // 128):
                        s_, half = i // 2, i % 2
                        b = nb()
                        for kc in range(KC):
                            PE(_f_mm(bank(b), HNV[:, kc, i * 128:(i + 1) * 128], WKV3[:, kc, :],
                                     kc == 0, kc == KC - 1),
                               HNK(kc, i * 128, (i + 1) * 128) + k_wkv, BK(b))
                        so = OX + 40960 + (i % 2) * 2048
                        stg = fv(so, 512)
                        ACT(_f_acopy(stg, bank(b)), BK(b), K(so, 2048))
                        if "vacopy" not in _KSKIP:
                            DVE(_f_copy(VAV[:, i, :, 0:64], bank(b)[:, 256:512].rearrange("p (h d) -> p h d", h=4)),
                                BK(b), VAK(i, i + 1))
                        if "kvout" in _KSKIP:
                            continue
                        DMA(_f_dma(nck_d[s_, l, :, half * 128:(half + 1) * 128, :].rearrange("h t d -> t h d"),
                                   stg[:, 0:256].rearrange("p (h d) -> p h d", h=4)),
                            K(so, 2048), [], "kvs%d" % (i % 2))
                        DMA(_f_dma(ncv_d[s_, l, :, half * 128:(half + 1) * 128, :].rearrange("h t d -> t h d"),
                                   stg[:, 256:512].rearrange("p (h d) -> p h d", h=4)),
                            K(so, 2048), [], "kvs%d" % (i % 2))

                _stage(5 + 10 * l + (50 if is_sample else 0))
                S0, S1, S2, S3, S4 = [slot_f(i, NT) for i in range(5)]
                XCB = bv(OWV, NT) if is_sample else bv(OX + 45056, NT)
                xcb_off = OWV if is_sample else OX + 45056

                def XCBK(t0, t1):
                    return K(xcb_off + t0 * 2, (t1 - t0) * 2)

                def s3d(v, lo, hi, P=None):
                    return v.rearrange("p (s t) -> p s t", s=nseq)[:, :, lo:hi]

                for c in range(2):
                    wv_, wk_ = load_wblock(l, 10 + c)
                    for j in range(NTL):
                        b = proj_tile(wv_, wk_, j)
                        ACT(_f_acopy(S1[:, j * 512:(j + 1) * 512], bank(b)), BK(b), SK(1, j * 512, (j + 1) * 512))
                    wv_, wk_ = load_wblock(l, 8 + c)
                    for j in range(NTL):
                        b = proj_tile(wv_, wk_, j)
                        DVE(_f_tt(S2[:, j * 512:(j + 1) * 512], bank(b), S1[:, j * 512:(j + 1) * 512], ALU.mult),
                            BK(b) + SK(1, j * 512, (j + 1) * 512), SK(2, j * 512, (j + 1) * 512))
                    wv_, wk_ = load_wblock(l, 6 + c)
                    for j in range(NTL):
                        b = proj_tile(wv_, wk_, j)
                        ACT(_f_acopy(S0[:, j * 512:(j + 1) * 512], bank(b)), BK(b), SK(0, j * 512, (j + 1) * 512))
                    cw = lambda tap: spc("convb", (l * 3 + tap) * 2 + c)
                    DVE(_f_ts(S3, S2, cw(1), ALU.mult), SK(2, 0, NT) + K_SP, SK(3, 0, NT))
                    DVE(_f_stt(s3d(S3, 1, T), s3d(S2, 0, T - 1), cw(0), s3d(S3, 1, T), ALU.mult, ALU.add),
                        SK(2, 0, NT) + SK(3, 0, NT) + K_SP, SK(3, 0, NT))
                    DVE(_f_stt(s3d(S3, 0, T - 1), s3d(S2, 1, T), cw(2), s3d(S3, 0, T - 1), ALU.mult, ALU.add),
                        SK(2, 0, NT) + SK(3, 0, NT) + K_SP, SK(3, 0, NT))
                    POOL(_f_tt(Y6V[:, c, :], S0, S3, ALU.mult), SK(0, 0, NT) + SK(3, 0, NT), Y6K(c, 0, NT))

                    wv_, wk_ = load_wblock(l, 12 + c)
                    for j in range(NTL):
                        b = proj_tile(wv_, wk_, j)
                        ACT(_f_acopy(S0[:, j * 512:(j + 1) * 512], bank(b)), BK(b), SK(0, j * 512, (j + 1) * 512))
                    ccw = lambda tap: spc("convc", (l * 4 + tap) * 2 + c)
                    DVE(_f_ts(S1, S0, ccw(2), ALU.mult, spc("convcb", l * 2 + c), ALU.add),
                        SK(0, 0, NT) + K_SP, SK(1, 0, NT))
                    for tap, sh in ((0, 2), (1, 1)):
                        DVE(_f_stt(s3d(S1, sh, T), s3d(S0, 0, T - sh), ccw(tap), s3d(S1, sh, T), ALU.mult, ALU.add),
                            SK(0, 0, NT) + SK(1, 0, NT) + K_SP, SK(1, 0, NT))
                    DVE(_f_stt(s3d(S1, 0, T - 1), s3d(S0, 1, T), ccw(3), s3d(S1, 0, T - 1), ALU.mult, ALU.add),
                        SK(0, 0, NT) + SK(1, 0, NT) + K_SP, SK(1, 0, NT))
                    POOL(_f_copy(XCB, S1), SK(1, 0, NT), XCBK(0, NT))
                    wv_, wk_ = load_wblock(l, 14 + c)
                    for j in range(NTL):
                        b = proj_tile(wv_, wk_, j)
                        ACT(_f_acopy(S0[:, j * 512:(j + 1) * 512], bank(b)), BK(b), SK(0, j * 512, (j + 1) * 512))
                    DVE(_f_tt(S2, S0, S0, ALU.mult), SK(0, 0, NT), SK(2, 0, NT))
                    DVE(_f_ts(S2, S2, 0.044715, ALU.mult, 1.0, ALU.add), SK(2, 0, NT), SK(2, 0, NT))
                    DVE(_f_tt(S2, S2, S0, ALU.mult), SK(2, 0, NT) + SK(0, 0, NT), SK(2, 0, NT))
                    ACT(_f_act(S2, S2, AF.Tanh, scale=GELU_K), SK(2, 0, NT), SK(2, 0, NT))
                    DVE(_f_stt(S0, S2, 1.0, S0, ALU.add, ALU.mult), SK(2, 0, NT) + SK(0, 0, NT), SK(0, 0, NT))
                    for d in range(2):
                        TMP, OUT = (S4, S4) if d == 0 else (S1, S1)
                        ti, oi = (4, 4) if d == 0 else (1, 1)
                        for j in range(NTL):
                            for g_, dst, di in ((0, S2, 2), (1, S3, 3)):
                                b = nb()
                                PE(_f_mm(bank(b), wsm(((d * 2 + g_) * 2 + c)), XCB[:, j * 512:(j + 1) * 512],
                                         True, True), K_WSM + XCBK(j * 512, (j + 1) * 512), BK(b))
                                ACT(_f_act(dst[:, j * 512:(j + 1) * 512], bank(b), AF.Sigmoid,
                                           bias=spc("rgb", ((l * 2 + d) * 2 + g_) * 2 + c)),
                                    BK(b) + K_SP, SK(di, j * 512, (j + 1) * 512))
                        DVE(_f_tt(S3, S3, S1, ALU.mult), SK(3, 0, NT) + SK(1, 0, NT), SK(3, 0, NT))
                        clc = misc(32 + (l * 2 + d) * 2 + c)
                        cl2c = misc(48 + (l * 2 + d) * 2 + c)
                        ACT(_f_act(TMP, S2, AF.Exp, scale=cl2c), SK(2, 0, NT) + K_MISC, SK(ti, 0, NT))
                        ACT(_f_act(S2, S2, AF.Exp, scale=clc), SK(2, 0, NT) + K_MISC, SK(2, 0, NT))
                        ACT(_f_act(TMP, TMP, AF.Sqrt, bias=1.0, scale=-1.0), SK(ti, 0, NT), SK(ti, 0, NT))
                        DVE(_f_tt(S3, S3, TMP, ALU.mult), SK(3, 0, NT) + SK(ti, 0, NT), SK(3, 0, NT))
                        for s_ in range(nseq):
                            init = spc("h0", (l * 2 + d) * 2 + c) if is_sample else 0.0
                            a0, a1 = s_ * T, (s_ + 1) * T
                            if d == 0:
                                oa, aa, ba_ = OUT[:, a0:a1], S2[:, a0:a1], S3[:, a0:a1]
                            else:
                                oa = OUT[:, a0:a1][:, ::-1]
                                aa = S2[:, a0:a1][:, ::-1]
                                ba_ = S3[:, a0:a1][:, ::-1]
                            DVE(_f_scan(oa, aa, ba_, init),
                                SK(2, a0, a1) + SK(3, a0, a1) + K_SP, SK(oi, a0, a1))
                            if not is_sample:
                                col = ((s_ * 2 + l) * 2 + d) * 2 + c
                                pos = a1 - 1 if d == 0 else a0
                                DVE(_f_copy(HSTV[:, col:col + 1], OUT[:, pos:pos + 1]), SK(oi, a0, a1), K_HST)
                    DVE(_f_tt(S4, S4, S1, ALU.add), SK(4, 0, NT) + SK(1, 0, NT), SK(4, 0, NT))
                    DVE(_f_stt(Y6V[:, 2 + c, :], S4, 0.5, S0, ALU.mult, ALU.mult),
                        SK(4, 0, NT) + SK(0, 0, NT), Y6K(2 + c, 0, NT))

                    PW = T + 24
                    P0 = slot_f(0, nseq * PW).rearrange("p (s t) -> p s t", s=nseq)
                    P1 = slot_f(1, nseq * PW).rearrange("p (s t) -> p s t", s=nseq)
                    P2 = slot_f(2, nseq * PW).rearrange("p (s t) -> p s t", s=nseq)
                    P3 = slot_f(3, nseq * PW).rearrange("p (s t) -> p s t", s=nseq)
                    P4 = slot_f(4, nseq * PW).rearrange("p (s t) -> p s t", s=nseq)
                    DVE(_f_memset(P0[:, :, 0:8], 0.0), [], SK(0))
                    DVE(_f_memset(P0[:, :, 8 + T:PW], 0.0), [], SK(0))
                    wv_, wk_ = load_wblock(l, 16 + c)
                    for j in range(NTL):
                        b = proj_tile(wv_, wk_, j)
                        for (s_, ta, tb, c0) in seq_pieces(j):
                            ACT(_f_acopy(P0[:, s_, 8 + ta:8 + tb], bank(b)[:, c0:c0 + (tb - ta)]), BK(b), SK(0))
                    DVE(_f_tt(P1[:, :, 0:PW - 1], P0[:, :, 0:PW - 1], P0[:, :, 1:PW], ALU.add), SK(0), SK(1))
                    if c == 0:
                        DVE(_f_tt(P2[64:128, :, 0:PW - 3], P1[64:128, :, 0:PW - 3], P1[64:128, :, 2:PW - 1], ALU.add),
                            SK(1), SK(2))
                        srcs = [(P1, 1, 7), (P2, 2, 6)]
                        R, ri = P3, 3
                    else:
                        DVE(_f_tt(P2[:, :, 0:PW - 3], P1[:, :, 0:PW - 3], P1[:, :, 2:PW - 1], ALU.add), SK(1), SK(2))
                        DVE(_f_tt(P3[:, :, 0:PW - 7], P2[:, :, 0:PW - 7], P2[:, :, 4:PW - 3], ALU.add), SK(2), SK(3))
                        DVE(_f_tt(P4[64:128, :, 0:PW - 15], P3[64:128, :, 0:PW - 15], P3[64:128, :, 8:PW - 7],
                                  ALU.add), SK(3), SK(4))
                        srcs = [(P3, 3, 4), (P4, 4, 0)]
                        R, ri = P1, 1
                    for hf, (src, si, sh) in enumerate(srcs):
                        pl, ph = hf * 64, hf * 64 + 64
                        pinv = SPV[pl:ph, SP["pinv"] + c:SP["pinv"] + c + 1]
                        DVE(_f_ts(R[pl:ph, :, 0:T], src[pl:ph, :, sh:sh + T], pinv, ALU.mult),
                            SK(si) + K_SP, SK(ri))
                        e0 = SP["pedge"] + c * 32
                        for s_ in range(nseq):
                            DVE(_f_tt(R[pl:ph, s_, 0:16], src[pl:ph, s_, sh:sh + 16], SPV[pl:ph, e0:e0 + 16],
                                      ALU.mult), SK(si) + K_SP, SK(ri))
                            DVE(_f_tt(R[pl:ph, s_, T - 16:T], src[pl:ph, s_, sh + T - 16:sh + T],
                                      SPV[pl:ph, e0 + 16:e0 + 32], ALU.mult), SK(si) + K_SP, SK(ri))
                    DVE(_f_tt(XCB.rearrange("p (s t) -> p s t", s=nseq), R[:, :, 0:T], P0[:, :, 8:8 + T],
                              ALU.subtract), SK(ri) + SK(0), XCBK(0, NT))
                    for j in range(NTL):
                        b = nb()
                        PE(_f_mm(bank(b), wsm(8 + c), XCB[:, j * 512:(j + 1) * 512], True, True),
                           K_WSM + XCBK(j * 512, (j + 1) * 512), BK(b))
                        ACT(_f_act(Y6V[:, 4 + c, j * 512:(j + 1) * 512], bank(b), AF.Identity,
                                   scale=spc("pscale", l * 2 + c)),
                            BK(b) + K_SP, Y6K(4 + c, j * 512, (j + 1) * 512))

                _stage(6 + 10 * l + (50 if is_sample else 0))
                rr["set"] = [0, 1, 2, 3]
                scale = 32.0 ** -0.5
                neglam = misc(l * 16 + 5)
                wsub = SPV[:, SP["wsub"] + l * 64:SP["wsub"] + l * 64 + 64]
                QN = min(512, T)
                NQS = QN // 128
                NQT = T // QN
                PT_off = [OSCR + 1 * SLOT, OSCR + 1 * SLOT + 2048]
                OTS_off = OSCR + 1 * SLOT + 4096
                YT_off = [OSCR + 2 * SLOT, OSCR + 2 * SLOT + 4096]
                OD_off = OSCR + 3 * SLOT
                DVE(_f_memset(bv(OSCR + 4 * SLOT, 4096), 0.0), [], K(OSCR + 4 * SLOT, 8192))
                units = [(s_, qt, h) for s_ in range(nseq) for qt in range(NQT) for h in range(4)]
                steps = [(u, kt) for u in range(len(units)) for kt in range(NK)]

                QM_off = OSCR + 4 * SLOT
                k_qm = K(QM_off, 8192)

                def qm_view(h, m):
                    return bv(QM_off + (h * 2 + m) * 1024, 512)

                def emit_qm(s_, qt):
                    q0 = s_ * T + qt * QN
                    for h in range(4):
                        pb, ch = (h % 2) * 64, h // 2
                        for m in range(2):
                            r0 = pb + 32 * m
                            v = qm_view(h, m)
                            if (h * 2 + m) % 2 == 0:
                                DVE(_f_copy(v[r0:r0 + 32, 0:QN], QTV[r0:r0 + 32, ch, q0:q0 + QN]),
                                    QTK(ch, q0, q0 + QN), K(QM_off + (h * 2 + m) * 1024, 1024))
                            else:
                                ACT(_f_acopy(v[r0:r0 + 32, 0:QN], QTV[r0:r0 + 32, ch, q0:q0 + QN]),
                                    QTK(ch, q0, q0 + QN), K(QM_off + (h * 2 + m) * 1024, 1024))

                def emit_S(ix):
                    u, kt = steps[ix]
                    s_, qt, h = units[u]
                    bi = ix % 2
                    ch = h // 2
                    if kt == 0 and h == 0:
                        emit_qm(s_, qt)
                    kcol = s_ * TK + kt * 128
                    for m in range(2):
                        PE(_f_mm(bank(2 * bi + m)[:, 0:QN], KTV[:, ch, kcol:kcol + 128], qm_view(h, m)[:, 0:QN],
                                 True, True),
                           KTK(ch, kcol, kcol + 128) + K(QM_off + (h * 2 + m) * 1024, 1024), BK(2 * bi + m))

                PT3_off = [OHN, OHN + 2048, OHN + 4096]

                def emit_exp(ix):
                    u, kt = steps[ix]
                    bi = ix % 2
                    Sv = PP[bi][:, :].rearrange("p (m q) -> p m q", m=2)
                    PT = bv(PT3_off[ix % 3], 1024).rearrange("p (m q) -> p m q", m=2)
                    ACT(_f_act(PT[:, :, 0:QN], Sv[:, :, 0:QN], AF.Exp, scale=scale),
                        BK(2 * bi) + BK(2 * bi + 1), K(PT3_off[ix % 3], 2048))

                def emit_pv(ix):
                    u, kt = steps[ix]
                    s_, qt, h = units[u]
                    PT = bv(PT3_off[ix % 3], 1024).rearrange("p (m q) -> p m q", m=2)
                    k_pt = K(PT3_off[ix % 3], 2048)
                    for m in range(2):
                        PE(_f_mm(bank(4 + m)[0:65, 0:QN], VAV[:, s_ * NK + kt, h, :], PT[:, m, 0:QN],
                                 kt == 0, kt == NK - 1),
                           k_pt + VAK(s_ * NK + kt, s_ * NK + kt + 1), BK(4 + m))

                OTS = fv(OTS_off, 1024).rearrange("p (m q) -> p m q", m=2)
                k_ots = K(OTS_off, 4096)

                def emit_post_copy(u):
                    DVE(_f_copy(OTS[0:65, :, 0:QN],
                                PP[2][0:65, :].rearrange("p (m q) -> p m q", m=2)[:, :, 0:QN]),
                        BK(4) + BK(5), k_ots)

                def emit_post_a(u):
                    s_, qt, h = units[u]
                    par = u % 2
                    for m in range(2):
                        for qs in range(NQS):
                            PE(_f_tr(bank(6 + m)[:, qs * 65:qs * 65 + 65], OTS[0:65, m, qs * 128:(qs + 1) * 128],
                                     IDF[0:65, 0:65]), k_ots + K_IDF, BK(6 + m))
                    k_a = K(OCB + C_ATT + par * 256, 256)
                    odb = OD_off + par * 4096
                    k_od = K(odb, 4096)
                    o1v = bank(6)[:, 0:NQS * 65].rearrange("p (q d) -> p q d", d=65)
                    o2v = bank(7)[:, 0:NQS * 65].rearrange("p (q d) -> p q d", d=65)
                    r1, r2, nl2, ss, lnv, rstd = [ATT[:, par * 64 + i * 8:par * 64 + i * 8 + NQS] for i in range(6)]
                    DVE(_f_recip(r1, o1v[:, :, 64]), BK(6), k_a)
                    DVE(_f_recip(r2, o2v[:, :, 64]), BK(7), k_a)
                    DVE(_f_ts(nl2, r2, neglam, ALU.mult), k_a + K_MISC, k_a)
                    for qs in range(NQS):
                        odo = odb + qs * 1024
                        od, t2, junk = fv(odo, 64), fv(odo + 256, 64), fv(odo + 512, 64)
                        DVE(_f_ts(t2, o2v[:, qs, 0:64], nl2[:, qs:qs + 1], ALU.mult), BK(7) + k_a, k_od)
                        DVE(_f_stt(od, o1v[:, qs, 0:64], r1[:, qs:qs + 1], t2, ALU.mult, ALU.add),
                            BK(6) + k_a + k_od, k_od)

                def emit_post_b(u):
                    s_, qt, h = units[u]
                    par = u % 2
                    q0 = s_ * T + qt * QN
                    yo = YT_off[(s_ * NQT + qt) % 2]
                    k_yt = K(yo, 4096)
                    k_a = K(OCB + C_ATT + par * 256, 256)
                    odb = OD_off + par * 4096
                    k_od = K(odb, 4096)
                    r1, r2, nl2, ss, lnv, rstd = [ATT[:, par * 64 + i * 8:par * 64 + i * 8 + NQS] for i in range(6)]
                    for qs in range(NQS):
                        odo = odb + qs * 1024
                        ACT(_f_act(fv(odo + 512, 64), fv(odo, 64), AF.Square, accum_out=ss[:, qs:qs + 1]),
                            k_od, k_od + k_a)
                    ACT(_f_act(lnv, ss, AF.Ln, bias=EPS, scale=1.0 / 64), k_a, k_a)
                    ACT(_f_act(rstd, lnv, AF.Exp, scale=-0.5), k_a, k_a)
                    for qs in range(NQS):
                        od = fv(odb + qs * 1024, 64)
                        YT = fv(yo + qs * 1024, 256)
                        DVE(_f_stt(YT[:, h * 64:(h + 1) * 64], od, rstd[:, qs:qs + 1], wsub, ALU.mult, ALU.mult),
                            k_od + k_a + K_SP, k_yt)
                    if h == 3:
                        for ch in range(2):
                            for qs in range(NQS):
                                YT = fv(yo + qs * 1024, 256)
                                PE(_f_tr(bank(6 + ch)[:, qs * 128:(qs + 1) * 128], YT[:, ch * 128:(ch + 1) * 128], IDF),
                                   k_yt + K_IDF, BK(6 + ch))
                            if ch == 0:
                                ACT(_f_acopy(YAV[:, ch, q0:q0 + QN], bank(6 + ch)[:, 0:QN]), BK(6 + ch),
                                    YAK(ch, q0, q0 + QN))
                            else:
                                DVE(_f_copy(YAV[:, ch, q0:q0 + QN], bank(6 + ch)[:, 0:QN]), BK(6 + ch),
                                    YAK(ch, q0, q0 + QN))

                emit_S(0)
                if len(steps) > 1:
                    emit_S(1)
                todo = []
                DA, DB = 1, (4 if NK >= 8 else 2)
                for ix in range(len(steps)):
                    emit_exp(ix)
                    if ix + 2 < len(steps):
                        emit_S(ix + 2)
                    emit_pv(ix)
                    for it in [t_ for t_ in todo if t_[0] <= ix]:
                        it[1](it[2])
                    todo = [t_ for t_ in todo if t_[0] > ix]
                    u, kt = steps[ix]
                    if kt == NK - 1:
                        emit_post_copy(u)
                        todo.append((ix + DA, emit_post_a, u))
                        todo.append((ix + DB, emit_post_b, u))
                for it in todo:
                    it[1](it[2])

                _stage(7 + 10 * l + (50 if is_sample else 0))
                rr["set"] = [4, 5, 6, 7]
                if (not is_sample) and l == 0 and DEPTH > 1:
                    rr["set"] = [4, 5, 6]
                    setup_layer(1, [OX + 32768 + i * 8192 for i in range(4)], 7)
                WO3 = bv(OSCR + 2 * SLOT, KC * D).rearrange("p (k n) -> p k n", k=KC)
                k_wo = K(OSCR + 2 * SLOT, 16384)
                DMA(_f_dma(WO3, woutb[l].rearrange("p (k n) -> p k n", k=KC)), [("dram", "wout%d" % l)], k_wo, "wo")
                HN2 = bv(OSCR + SLOT, KC * 512).rearrange("p (c t) -> p c t", c=KC)
                HV = bv(OHN, 32 * 512).rearrange("p (h t) -> p h t", h=32)
                W1_off = [OKT, OKT + 8192]
                W2_off = [OKT + 16384, OKT + 20480]
                RT_off = [OWS + 4096, OWS + 6144]
                w1c = {"i": 0}
                w2c = {"i": 0}
                def st5_wout(j):
                    t0, t1 = j * 512, (j + 1) * 512
                    for oc in range(8):
                        b = nb()
                        for kc in range(KC):
                            if kc < 2:
                                rhs, rk = YAV[:, kc, t0:t1], YAK(kc, t0, t1)
                            else:
                                rhs, rk = Y6V[:, kc - 2, t0:t1], Y6K(kc - 2, t0, t1)
                            PE(_f_mm(bank(b), WO3[:, kc, oc * 128:(oc + 1) * 128], rhs, kc == 0, kc == KC - 1),
                               k_wo + rk, BK(b))
                        DVE(_f_stt(XV[:, oc, t0:t1], bank(b), modc(l, 2, oc, jm), XV[:, oc, t0:t1], ALU.mult, ALU.add),
                            BK(b) + K_MOD + XK(oc, t0, t1), XK(oc, t0, t1))

                def hn2_out(c, tt, tk, sh):
                    DVE(_f_ts(HN2[:, c, :], tt, sh, ALU.add), tk + K_MOD, K(OSCR + SLOT + c * 1024, 1024))

                def st5_mlp1(j):
                    for hg in range(8):
                        i1 = w1c["i"]
                        w1c["i"] = (i1 + 1) % 2
                        W1v = bv(W1_off[i1], 4 * KC * 128).rearrange("p (h k n) -> p h k n", h=4, k=KC)
                        k_w1 = K(W1_off[i1], 8192)
                        DMA(_f_dma(W1v.rearrange("p h k n -> p h (k n)"),
                                   w1b[l, hg * 512:(hg + 1) * 512, :].rearrange("(h p) n -> p h n", p=128)),
                            [("dram", "w1%d" % l)], k_w1, "w1s%d" % i1)
                        for hh in range(4):
                            hc = hg * 4 + hh
                            b = nb()
                            for kc in range(KC):
                                PE(_f_mm(bank(b), W1v[:, hh, kc, :], HN2[:, kc, :], kc == 0, kc == KC - 1),
                                   k_w1 + K(OSCR + SLOT + kc * 1024, 1024), BK(b))
                            ro = RT_off[hc % 2]
                            rt = fv(ro, 512)
                            ACT(_f_act(rt, bank(b), AF.Square), BK(b), K(ro, 2048))
                            DVE(_f_stt(HV[:, hc, :], bank(b), 0.0, rt, ALU.is_gt, ALU.mult),
                                BK(b) + K(ro, 2048), K(OHN + hc * 1024, 1024))

                def st5_mlp2_half(j, half):
                    t0, t1 = j * 512, (j + 1) * 512
                    for hg in range(8):
                        i2 = w2c["i"]
                        w2c["i"] = (i2 + 1) % 2
                        W2v = bv(W2_off[i2], 4 * 512).rearrange("p (h n) -> p h n", h=4)
                        k_w2 = K(W2_off[i2], 4096)
                        DMA(_f_dma(W2v, w2b[l, hg * 512:(hg + 1) * 512, half * 512:(half + 1) * 512]
                                   .rearrange("(h p) n -> p h n", p=128)),
                            [("dram", "w2%d" % l)], k_w2, "w2s%d" % i2)
                        for hh in range(4):
                            hc = hg * 4 + hh
                            for o4 in range(4):
                                PE(_f_mm(bank(o4), W2v[:, hh, o4 * 128:(o4 + 1) * 128], HV[:, hc, :],
                                         hc == 0, hc == 31),
                                   k_w2 + K(OHN + hc * 1024, 1024), BK(o4))
                    for o4 in range(4):
                        oc = half * 4 + o4
                        DVE(_f_stt(XV[:, oc, t0:t1], bank(o4), modc(l, 5, oc, jm), XV[:, oc, t0:t1],
                                   ALU.mult, ALU.add),
                            BK(o4) + K_MOD + XK(oc, t0, t1), XK(oc, t0, t1))

                skip_mlp = _KNOMLP and l == min(DEPTH, _KLAYERS) - 1
                for j in range(NTL):
                    st5_wout(j)
                    if skip_mlp:
                        continue
                    pre = (lambda jj=j: st5_mlp2_half(jj - 1, 0)) if j > 0 else None
                    post = (lambda jj=j: st5_mlp2_half(jj - 1, 1)) if j > 0 else None
                    norm_tile(j, lambda c: gc_(l, 1, c, jm), lambda c: modc(l, 3, c, jm), hn2_out, NTMP5,
                              pe_pre=pre, pe_post=post)
                    st5_mlp1(j)
                if not skip_mlp:
                    st5_mlp2_half(NTL - 1, 0)
                    st5_mlp2_half(NTL - 1, 1)

            _stage(30 + (50 if is_sample else 0))
            rr["set"] = [0, 1, 2, 3, 4, 5]
            oc_ = {"i": 0}
            for j in range(NTL):
                t0 = j * 512
                FT = fv(OHN, 8 * 512).rearrange("p (c t) -> p c t", c=8)

                def fin_out(c, tt, tk, sh):
                    POOL(_f_copy(FT[:, c, :], tt), tk, K(OHN + c * 2048, 2048))

                if _KRAW:
                    for c in range(8):
                        DVE(_f_copy(FT[:, c, :], XV[:, c, t0:t0 + 512]), XK(c, t0, t0 + 512), K(OHN + c * 2048, 2048))
                else:
                    norm_tile(j, lambda c: spc("fnw", c), None, fin_out, OSCR)
                for i4 in range(4):
                    oi = oc_["i"]
                    oc_["i"] = (oi + 1) % 2
                    so = OY6 + oi * 4096
                    ost = fv(so, 1024)
                    for half in range(2):
                        b = nb()
                        for q in range(4):
                            c = half * 4 + q
                            PE(_f_tr(bank(b)[:, q * 128:(q + 1) * 128], FT[:, c, i4 * 128:(i4 + 1) * 128], IDF),
                               K(OHN + c * 2048, 2048) + K_IDF, BK(b))
                        if half == 0:
                            ACT(_f_acopy(ost[:, 0:512], bank(b)), BK(b), K(so, 2048))
                        else:
                            DVE(_f_copy(ost[:, 512:1024], bank(b)), BK(b), K(so + 2048, 2048))
                    r0 = t0 + i4 * 128
                    DMA(_f_dma(y_d[r0:r0 + 128, :], ost), K(so, 4096), [], "os%d" % oi)
            if not is_sample:
                b = nb()
                PE(_f_tr(bank(b)[0:32, 0:128], HSTV, IDF), K_HST + K_IDF, BK(b))
                ho = OY6 + 8192
                hsb = fv(ho, 128)
                ACT(_f_acopy(hsb[0:32, :], bank(b)[0:32, 0:128]), BK(b), K(ho, 512))
                DMA(_f_dma(nst_d[:, :], hsb[0:32, :]), K(ho, 512), [], "hst")

        NTMP5 = OSCR + 4 * SLOT
        try:
            _stage(0)
            run_batch(False)
            run_batch(True)
        except _Stop:
            pass
        S.emit(nc, es)
        _CACHE["sched"] = S
    return nc


def _rope_tables():
    f = np.arange(128)
    d = f % 32
    axis = d // 16
    e = d % 16
    i = e % 8
    freq = (10000.0 ** (-(i.astype(np.float64)) / 8.0))
    t = np.arange(2048)
    rows, cols = t // 64, t % 64
    pos = np.where(axis[:, None] == 0, rows[None, :], cols[None, :]).astype(np.float64)
    ang = (pos.astype(np.float32) * freq.astype(np.float32)[:, None]).astype(np.float32)
    c = np.cos(ang).astype(np.float32)
    s = np.sin(ang).astype(np.float32)
    sgn = np.where(e < 8, -1.0, 1.0).astype(np.float32)[:, None]
    return np.ascontiguousarray(c), np.ascontiguousarray(s * sgn)


def _rope_partner_cols():
    f = np.arange(256)
    e = f % 16
    return np.where(e < 8, f + 8, f - 8)


def _cm(v):
    v = np.asarray(v, np.float32)
    return np.ascontiguousarray(v.reshape(-1, 128).T)


def _blk(w):
    n = w.shape[1] // 128
    return np.ascontiguousarray(w.reshape(KC, 128, n, 128).transpose(2, 1, 0, 3).reshape(n * 128, KC * 128))


def _prep_shared(inp):
    f32 = np.float32
    sh = {}
    w_in = np.asarray(inp["w_in"], f32)
    pc = _rope_partner_cols()
    win = []
    wkv = []
    for l in range(DEPTH):
        w = w_in[l]
        ext = np.concatenate([w, w[:, 0:256][:, pc], w[:, 256:512][:, pc]], axis=1)
        win.append(_blk(ext))
        wkv.append(np.ascontiguousarray(
            w[:, 256:768].reshape(KC, 128, 512).transpose(1, 0, 2).reshape(128, KC * 512)))
    sh["win"] = np.stack(win)
    sh["wkv"] = np.stack(wkv)
    rg_w = np.asarray(inp["rg_w"], f32)
    pool_w = np.asarray(inp["pool_w"], f32)
    wsm = np.zeros((DEPTH, 128, 10, 128), f32)
    for l in range(DEPTH):
        for d in range(2):
            for g in range(2):
                for c in range(2):
                    i = (d * 2 + g) * 2 + c
                    for bb in range(2):
                        wsm[l, bb * 64:(bb + 1) * 64, i, bb * 64:(bb + 1) * 64] = rg_w[l, d, g, c * 2 + bb]
        for c in range(2):
            for bb in range(2):
                wsm[l, bb * 64:(bb + 1) * 64, 8 + c, bb * 64:(bb + 1) * 64] = pool_w[l, c * 2 + bb]
    sh["wsm"] = wsm.reshape(DEPTH, 128, 1280)
    w_out = np.asarray(inp["w_out"], f32)
    sh["wout"] = np.ascontiguousarray(
        w_out.reshape(DEPTH, KC, 128, D).transpose(0, 2, 1, 3).reshape(DEPTH, 128, KC * D))
    w1 = np.asarray(inp["w_mlp1"], f32)
    sh["w1"] = np.stack([_blk(w1[l]) for l in range(DEPTH)])
    w2 = np.asarray(inp["w_mlp2"], f32)
    sh["w2"] = np.ascontiguousarray(w2)
    wada = np.asarray(inp["w_ada"], f32)
    sh["wada"] = np.ascontiguousarray(
        wada.reshape(DEPTH, KC, 128, 12, 512).transpose(0, 3, 2, 1, 4))
    rc, rs = _rope_tables()
    sh["ropec"], sh["ropes"] = rc, rs
    sh["ident"] = np.eye(128, dtype=f32)
    return sh


def _sp_pack(inp, core):
    f32 = np.float32
    sp = np.zeros((128, NSP), f32)

    def put(name, arr):
        arr = np.asarray(arr, f32)
        sp[:, SP[name]:SP[name] + arr.shape[1]] = arr

    put("c", _cm(inp["c"][core]))
    put("cctx", _cm(inp["c_ctx"]))
    put("bada", np.concatenate([_cm(inp["b_ada"][l]) for l in range(DEPTH)], axis=1))
    put("normw", np.concatenate([_cm(inp["norm_w"][l, i]) for l in range(DEPTH) for i in range(2)], axis=1))
    put("fnw", _cm(inp["final_norm_w"]))
    put("convb", np.concatenate([_cm(inp["conv_b_w"][l, t]) for l in range(DEPTH) for t in range(3)], axis=1))
    put("convc", np.concatenate([_cm(inp["conv_c_w"][l, t]) for l in range(DEPTH) for t in range(4)], axis=1))
    put("convcb", np.concatenate([_cm(inp["conv_c_b"][l]) for l in range(DEPTH)], axis=1))
    put("rgb", np.concatenate([_cm(inp["rg_b"][l, d, g]) for l in range(DEPTH) for d in range(2)
                               for g in range(2)], axis=1))
    put("rglam", np.concatenate([_cm(inp["rg_lambda"][l, d]) for l in range(DEPTH) for d in range(2)], axis=1))
    put("pscale", np.concatenate([_cm(inp["pool_scale"][l]) for l in range(DEPTH)], axis=1))
    put("h0", np.concatenate([_cm(inp["state_rglru"][core, l, d]) for l in range(DEPTH) for d in range(2)], axis=1))
    wins = (2, 4, 8, 16)
    pinv = np.zeros((128, 2), f32)
    pedge = np.zeros((128, 64), f32)
    for c in range(2):
        for hf in range(2):
            w = wins[2 * c + hf]
            left = w // 2
            right = w - 1 - left
            pinv[hf * 64:(hf + 1) * 64, c] = 1.0 / w
            for i in range(16):
                cnt_first = i + right - max(i - left, 0) + 1
                cnt_last = min(right, 15 - i) + left + 1
                pedge[hf * 64:(hf + 1) * 64, c * 32 + i] = 1.0 / cnt_first
                pedge[hf * 64:(hf + 1) * 64, c * 32 + 16 + i] = 1.0 / cnt_last
    put("pinv", pinv)
    put("pedge", pedge)
    dl = np.asarray(inp["diff_lambda"], f32).reshape(DEPTH * 128)
    put("dl", np.broadcast_to(dl[None, :], (128, DEPTH * 128)))
    ws = np.asarray(inp["subln_w"], f32).reshape(DEPTH * 64)
    put("wsub", np.broadcast_to(ws[None, :], (128, DEPTH * 64)))
    return sp


_CACHE = {}
import os
_KSTOP = int(os.environ.get("KSTOP", "999"))
_MUTE_INIT = True


class _Stop(Exception):
    pass


_DEBUG = os.environ.get("KDEBUG", "") == "1"
_KLAYERS = int(os.environ.get("KLAYERS", "2"))
_KRAW = os.environ.get("KRAW", "") == "1"
_KNOMLP = os.environ.get("KNOMLP", "") == "1"
_MUTE = {"on": False}
_KSKIP = set(os.environ.get("KSKIP", "").split(","))


def _stage(n):
    if n >= _KSTOP:
        if n < 0:
            _MUTE["on"] = True
        else:
            raise _Stop()


def kernel(**inputs):
    f32 = np.float32
    if "nc" not in _CACHE:
        _CACHE["nc"] = build_program()
    nc = _CACHE["nc"]
    sh = _prep_shared(inputs)
    x_prompt = np.asarray(inputs["x_prompt"], f32)
    x_sample = np.asarray(inputs["x_sample"], f32)
    cache_k = np.asarray(inputs["cache_k"], f32)
    cache_v = np.asarray(inputs["cache_v"], f32)
    in_maps = []
    for i in range(8):
        m = dict(sh)
        m["xs"] = np.ascontiguousarray(x_sample[i])
        m["xp"] = np.ascontiguousarray(x_prompt[4 * i:4 * i + 4].reshape(1024, D))
        m["ck"] = np.ascontiguousarray(cache_k[i])
        m["cv"] = np.ascontiguousarray(cache_v[i])
        m["sp"] = _sp_pack(inputs, i)
        in_maps.append(m)
    res = run_bass_kernel_spmd(nc, in_maps, core_ids=list(range(8)))
    outs = res.results
    y_prompt = np.concatenate([np.asarray(o["yp"], f32).reshape(4, 256, D) for o in outs], axis=0)
    y_sample = np.stack([np.asarray(o["ys"], f32) for o in outs], axis=0)
    nck = np.concatenate([np.asarray(o["nck"], f32) for o in outs], axis=0)
    ncv = np.concatenate([np.asarray(o["ncv"], f32) for o in outs], axis=0)
    nst = np.concatenate([np.asarray(o["nst"], f32).reshape(4, DEPTH, 2, 256) for o in outs], axis=0)
    return (y_prompt, y_sample, nck, ncv, nst)
```
